# Optimizing a Trainium2 kernel written in Bass

```python
import math
import jax, jax.numpy as jnp
from jax import lax
import numpy as np

D_MODEL = 1024
BATCH = 16
SEQ = 4096
DEPTH = 1
DEC_BATCH = 8
DEC_SEQ = 16
PAST_LEN = 2048

CHUNK = 64
EPS = 1e-6
N_MOD = 6
HG_HEADS = 8
HG_DK = 128
HG_DV = D_MODEL // HG_HEADS
HG_WIDTH = HG_HEADS * HG_DV
HG_KW = HG_HEADS * HG_DK
SSM_EXPAND = 2
SSM_D_INNER = SSM_EXPAND * D_MODEL
SSM_HEADDIM = 64
SSM_HEADS = SSM_D_INNER // SSM_HEADDIM
SSM_GROUPS = 4
SSM_HPG = SSM_HEADS // SSM_GROUPS
SSM_STATE = 128
SSM_CONV = 4
SSM_XBC = SSM_D_INNER + 2 * SSM_GROUPS * SSM_STATE
IN_SIZES = (HG_KW, HG_KW, HG_WIDTH, HG_WIDTH, SSM_D_INNER, SSM_XBC, SSM_HEADS, D_MODEL, D_MODEL)
IN_COLS = sum(IN_SIZES)
PEER_HEADS = 8
PEER_NKEYS = 128
PEER_EXPERTS = PEER_NKEYS * PEER_NKEYS
PEER_TOPK = 16
PEER_DK = 256
PEER_DK_HALF = PEER_DK // 2
PEER_BLOCK = 256

kernel_name = 'hgrn2_mamba2_peer_adaln_stream_step'


def _rms(x):
    xf = x.astype(jnp.float32)
    return xf * lax.rsqrt(jnp.mean(xf * xf, axis=-1, keepdims=True) + EPS)


def _to_blocks(a, blk):
    b, t = a.shape[0], a.shape[1]
    return jnp.moveaxis(a.reshape((b, t // blk, blk) + a.shape[2:]), 1, 0)


def _from_blocks(a):
    n, b, l = a.shape[0], a.shape[1], a.shape[2]
    return jnp.moveaxis(a, 0, 1).reshape((b, n * l) + a.shape[3:])


def _gla_chunked(q, k, v, logf, s0, blk):
    causal = jnp.tril(jnp.ones((blk, blk), dtype=bool))[None, :, :, None, None]

    def step(s, inp):
        qb, kb, vb, gb = inp
        g = jnp.cumsum(gb, axis=1)
        decay = jnp.exp(jnp.where(causal, g[:, :, None] - g[:, None, :], -jnp.inf))
        att = jnp.einsum('bihk,bjhk,bijhk->bhij', qb, kb, decay)
        o = (jnp.einsum('bhij,bjhv->bihv', att, vb)
             + jnp.einsum('bihk,bhkv->bihv', qb * jnp.exp(g), s))
        g_last = g[:, -1]
        s = (jnp.exp(g_last)[..., None] * s
             + jnp.einsum('bjhk,bjhv->bhkv', kb * jnp.exp(g_last[:, None] - g), vb))
        return s, o

    s_fin, o = lax.scan(step, s0, (_to_blocks(q, blk), _to_blocks(k, blk),
                                   _to_blocks(v, blk), _to_blocks(logf, blk)))
    return _from_blocks(o), s_fin


def _ssd_chunked(x, dt, a_neg, bm, cm, s0, blk):
    bn, t = x.shape[0], x.shape[1]
    xg = x.reshape(bn, t, SSM_GROUPS, SSM_HPG, SSM_HEADDIM)
    dtg = dt.reshape(bn, t, SSM_GROUPS, SSM_HPG)
    ag = (dt * a_neg).reshape(bn, t, SSM_GROUPS, SSM_HPG)
    h0 = s0.reshape(bn, SSM_GROUPS, SSM_HPG, SSM_HEADDIM, SSM_STATE)
    causal = jnp.tril(jnp.ones((blk, blk), dtype=bool))

    def step(h, inp):
        xb, dtb, ab, bb, cb = inp
        acum = jnp.cumsum(ab, axis=1)
        ac = jnp.transpose(acum, (0, 2, 3, 1))
        dtt = jnp.transpose(dtb, (0, 2, 3, 1))
        seg = jnp.exp(jnp.where(causal, ac[..., :, None] - ac[..., None, :], -jnp.inf))
        cbm = jnp.einsum('bign,bjgn->bgij', cb, bb)
        w = cbm[:, :, None] * seg * dtt[..., None, :]
        y = (jnp.einsum('bgrij,bjgrp->bigrp', w, xb)
             + jnp.einsum('bign,bgrpn->bigrp', cb, h) * jnp.exp(acum)[..., None])
        a_last = ac[..., -1]
        wd = jnp.exp(a_last[..., None] - ac) * dtt
        h = (jnp.exp(a_last)[..., None, None] * h
             + jnp.einsum('bjgn,bgrj,bjgrp->bgrpn', bb, wd, xb))
        return h, y

    h_fin, y = lax.scan(step, h0, (_to_blocks(xg, blk), _to_blocks(dtg, blk), _to_blocks(ag, blk),
                                   _to_blocks(bm, blk), _to_blocks(cm, blk)))
    y = _from_blocks(y).reshape(bn, t, SSM_HEADS, SSM_HEADDIM)
    return y, h_fin.reshape(bn, SSM_HEADS, SSM_HEADDIM, SSM_STATE)


def _token_mixer(h, s_hg, s_ssm, conv_buf, lb, w_in, hg_norm_w, conv_w, conv_b, dt_bias, a_log,
                 ssm_d, ssm_norm_w, w_branch_a, w_branch_b, w_out):
    f32 = jnp.float32
    bn, t = h.shape[0], h.shape[1]
    blk = min(CHUNK, t)
    proj = h @ w_in
    cuts = []
    acc = 0
    for s in IN_SIZES[:-1]:
        acc += s
        cuts.append(acc)
    q_r, f_r, i_r, g_r, z, xbc, dt_r, ga_r, gb_r = jnp.split(proj, cuts, axis=-1)

    q = jax.nn.silu(q_r.astype(f32)).reshape(bn, t, HG_HEADS, HG_DK)
    ff = f_r.astype(f32)
    logf = jnp.log(lb + (1.0 - lb) * jax.nn.sigmoid(ff)).reshape(bn, t, HG_HEADS, HG_DK)
    k = ((1.0 - lb) * jax.nn.sigmoid(-ff)).reshape(bn, t, HG_HEADS, HG_DK)
    v = i_r.astype(f32).reshape(bn, t, HG_HEADS, HG_DV)
    o, s_hg_new = _gla_chunked(q, k, v, logf, s_hg.astype(f32), blk)
    o = (_rms(o) * hg_norm_w.astype(f32).reshape(HG_HEADS, HG_DV)).reshape(bn, t, HG_WIDTH)
    o = (o * jax.nn.silu(g_r.astype(f32))).astype(h.dtype)
    pa = o @ w_branch_a

    xp = jnp.concatenate([conv_buf.astype(xbc.dtype), xbc], axis=1)
    conv_new = xp[:, xp.shape[1] - (SSM_CONV - 1):]
    xc = conv_b + xp[:, 0:t] * conv_w[0]
    for j in range(1, SSM_CONV):
        xc = xc + xp[:, j:j + t] * conv_w[j]
    xc = jax.nn.silu(xc.astype(f32))
    xs, bm, cm = jnp.split(xc, [SSM_D_INNER, SSM_D_INNER + SSM_GROUPS * SSM_STATE], axis=-1)
    xs = xs.reshape(bn, t, SSM_HEADS, SSM_HEADDIM)
    bm = bm.reshape(bn, t, SSM_GROUPS, SSM_STATE)
    cm = cm.reshape(bn, t, SSM_GROUPS, SSM_STATE)
    dt = jax.nn.softplus(dt_r.astype(f32) + dt_bias.astype(f32))
    a_neg = -jnp.exp(a_log.astype(f32))
    y, s_ssm_new = _ssd_chunked(xs, dt, a_neg, bm, cm, s_ssm.astype(f32), blk)
    y = y + ssm_d.astype(f32)[:, None] * xs
    y = y.reshape(bn, t, SSM_D_INNER) * jax.nn.silu(z.astype(f32))
    y = _rms(y.reshape(bn, t, SSM_GROUPS, SSM_D_INNER // SSM_GROUPS)).reshape(bn, t, SSM_D_INNER)
    y = (y * ssm_norm_w.astype(f32)).astype(h.dtype)
    pb = y @ w_branch_b

    merged = jax.nn.sigmoid(ga_r) * pa + jax.nn.sigmoid(gb_r) * pb
    return (merged @ w_out, s_hg_new.astype(s_hg.dtype), s_ssm_new.astype(s_ssm.dtype),
            conv_new.astype(conv_buf.dtype))


def _peer(h, wq, keys1, keys2, u, v):
    f32 = jnp.float32
    bn, t, d = h.shape
    n = bn * t
    blk = min(PEER_BLOCK, n)
    pad = (-n) % blk
    tok = jnp.pad(h.reshape(n, d), ((0, pad), (0, 0))).reshape(-1, blk, d)
    k1 = keys1.astype(f32)
    k2 = keys2.astype(f32)

    def one(tb):
        q = (tb @ wq).astype(f32).reshape(blk, PEER_HEADS, 2, PEER_DK_HALF)
        s1 = jnp.einsum('thd,hkd->thk', q[:, :, 0], k1)
        s2 = jnp.einsum('thd,hkd->thk', q[:, :, 1], k2)
        v1, i1 = lax.top_k(s1, PEER_TOPK)
        v2, i2 = lax.top_k(s2, PEER_TOPK)
        cand = (v1[..., :, None] + v2[..., None, :]).reshape(blk, PEER_HEADS, PEER_TOPK * PEER_TOPK)
        cidx = (i1[..., :, None] * PEER_NKEYS + i2[..., None, :]).reshape(blk, PEER_HEADS, PEER_TOPK * PEER_TOPK)
        sv, si = lax.top_k(cand, PEER_TOPK)
        eidx = jnp.take_along_axis(cidx, si, axis=-1)
        gw = jax.nn.softmax(sv, axis=-1)
        ue = jnp.take(u, eidx, axis=0)
        act = jax.nn.gelu(jnp.einsum('td,thed->the', tb, ue).astype(f32), approximate=False)
        ve = jnp.take(v, eidx, axis=0)
        return jnp.einsum('the,thed->td', (gw * act).astype(tb.dtype), ve)

    out = lax.map(one, tok).reshape(-1, d)[:n]
    return out.reshape(bn, t, d)


def _layer(x, c, s_hg, s_ssm, conv_buf, lb, w_ada, b_ada, norm1_w, w_in, hg_norm_w, conv_w, conv_b,
           dt_bias, a_log, ssm_d, ssm_norm_w, w_branch_a, w_branch_b, w_out, norm2_w,
           peer_wq, peer_keys1, peer_keys2, peer_u, peer_v):
    mod = (jax.nn.silu(c) @ w_ada + b_ada)[:, None, :]
    sh1, sc1, g1, sh2, sc2, g2 = jnp.split(mod, N_MOD, axis=-1)
    h = (_rms(x) * norm1_w * (1.0 + sc1) + sh1).astype(x.dtype)
    mix, s_hg_new, s_ssm_new, conv_new = _token_mixer(
        h, s_hg, s_ssm, conv_buf, lb, w_in, hg_norm_w, conv_w, conv_b, dt_bias, a_log,
        ssm_d, ssm_norm_w, w_branch_a, w_branch_b, w_out)
    x = x + g1 * mix
    h2 = (_rms(x) * norm2_w * (1.0 + sc2) + sh2).astype(x.dtype)
    x = x + g2 * _peer(h2, peer_wq, peer_keys1, peer_keys2, peer_u, peer_v)
    return x, s_hg_new, s_ssm_new, conv_new


def setup_inputs(seed: int = 0) -> dict:
    key = jax.random.key(seed)
    ks = jax.random.split(key, 32)
    nrm = jax.random.normal
    f32 = jnp.float32
    dt0 = jnp.exp(jax.random.uniform(ks[14], (DEPTH, SSM_HEADS), minval=math.log(1e-3), maxval=math.log(1e-1)))
    return {
        'x_prompt': nrm(ks[0], (BATCH, SEQ, D_MODEL), f32),
        'x_sample': nrm(ks[1], (DEC_BATCH, DEC_SEQ, D_MODEL), f32),
        'c_prompt': nrm(ks[2], (BATCH, D_MODEL), f32),
        'c_sample': nrm(ks[3], (DEC_BATCH, D_MODEL), f32),
        'state_hgrn': 0.5 * nrm(ks[4], (DEPTH, DEC_BATCH, HG_HEADS, HG_DK, HG_DV), f32),
        'state_ssm': 0.1 * nrm(ks[5], (DEPTH, DEC_BATCH, SSM_HEADS, SSM_HEADDIM, SSM_STATE), f32),
        'state_conv': nrm(ks[6], (DEPTH, DEC_BATCH, SSM_CONV - 1, SSM_XBC), f32),
        'w_ada': 0.3 * D_MODEL ** -0.5 * nrm(ks[7], (DEPTH, D_MODEL, N_MOD * D_MODEL), f32),
        'b_ada': 0.02 * nrm(ks[8], (DEPTH, N_MOD * D_MODEL), f32),
        'norm1_w': 1.0 + 0.02 * nrm(ks[9], (DEPTH, D_MODEL), f32),
        'w_in': D_MODEL ** -0.5 * nrm(ks[10], (DEPTH, D_MODEL, IN_COLS), f32),
        'hgrn_lower_bounds': 0.1 * nrm(ks[11], (DEPTH + 1, HG_KW), f32),
        'hgrn_norm_w': 1.0 + 0.02 * nrm(ks[12], (DEPTH, HG_WIDTH), f32),
        'conv_w': SSM_CONV ** -0.5 * nrm(ks[13], (DEPTH, SSM_CONV, SSM_XBC), f32),
        'conv_b': 0.02 * nrm(ks[15], (DEPTH, SSM_XBC), f32),
        'dt_bias': dt0 + jnp.log(-jnp.expm1(-dt0)),
        'a_log': jnp.log(jax.random.uniform(ks[16], (DEPTH, SSM_HEADS), minval=1.0, maxval=16.0)),
        'ssm_d': 1.0 + 0.02 * nrm(ks[17], (DEPTH, SSM_HEADS), f32),
        'ssm_norm_w': 1.0 + 0.02 * nrm(ks[18], (DEPTH, SSM_D_INNER), f32),
        'w_branch_a': HG_WIDTH ** -0.5 * nrm(ks[19], (DEPTH, HG_WIDTH, D_MODEL), f32),
        'w_branch_b': SSM_D_INNER ** -0.5 * nrm(ks[20], (DEPTH, SSM_D_INNER, D_MODEL), f32),
        'w_out': D_MODEL ** -0.5 * nrm(ks[21], (DEPTH, D_MODEL, D_MODEL), f32),
        'norm2_w': 1.0 + 0.02 * nrm(ks[22], (DEPTH, D_MODEL), f32),
        'peer_wq': D_MODEL ** -0.5 * nrm(ks[23], (DEPTH, D_MODEL, PEER_HEADS * PEER_DK), f32),
        'peer_keys1': PEER_DK_HALF ** -0.5 * nrm(ks[24], (DEPTH, PEER_HEADS, PEER_NKEYS, PEER_DK_HALF), f32),
        'peer_keys2': PEER_DK_HALF ** -0.5 * nrm(ks[25], (DEPTH, PEER_HEADS, PEER_NKEYS, PEER_DK_HALF), f32),
        'peer_u': D_MODEL ** -0.5 * nrm(ks[26], (DEPTH, PEER_EXPERTS, D_MODEL), f32),
        'peer_v': PEER_HEADS ** -0.5 * nrm(ks[27], (DEPTH, PEER_EXPERTS, D_MODEL), f32),
        'final_norm_w': 1.0 + 0.02 * nrm(ks[28], (D_MODEL,), f32),
    }


def reference(x_prompt, x_sample, c_prompt, c_sample, state_hgrn, state_ssm, state_conv,
              w_ada, b_ada, norm1_w, w_in, hgrn_lower_bounds, hgrn_norm_w, conv_w, conv_b,
              dt_bias, a_log, ssm_d, ssm_norm_w, w_branch_a, w_branch_b, w_out, norm2_w,
              peer_wq, peer_keys1, peer_keys2, peer_u, peer_v, final_norm_w):
    lb_all = jnp.cumsum(jax.nn.softmax(hgrn_lower_bounds.astype(jnp.float32), axis=0), axis=0)
    xp, xs = x_prompt, x_sample
    bp = x_prompt.shape[0]
    hg_p, ssm_p, conv_p, hg_s, ssm_s, conv_s = [], [], [], [], [], []
    for l in range(DEPTH):
        wl = (lb_all[l], w_ada[l], b_ada[l], norm1_w[l], w_in[l], hgrn_norm_w[l], conv_w[l], conv_b[l],
              dt_bias[l], a_log[l], ssm_d[l], ssm_norm_w[l], w_branch_a[l], w_branch_b[l], w_out[l],
              norm2_w[l], peer_wq[l], peer_keys1[l], peer_keys2[l], peer_u[l], peer_v[l])
        z_hg = jnp.zeros((bp, HG_HEADS, HG_DK, HG_DV), x_prompt.dtype)
        z_ssm = jnp.zeros((bp, SSM_HEADS, SSM_HEADDIM, SSM_STATE), x_prompt.dtype)
        z_conv = jnp.zeros((bp, SSM_CONV - 1, SSM_XBC), x_prompt.dtype)
        xp, a1, a2, a3 = _layer(xp, c_prompt, z_hg, z_ssm, z_conv, *wl)
        xs, b1, b2, b3 = _layer(xs, c_sample, state_hgrn[l], state_ssm[l], state_conv[l], *wl)
        hg_p.append(a1); ssm_p.append(a2); conv_p.append(a3)
        hg_s.append(b1); ssm_s.append(b2); conv_s.append(b3)
    y_prompt = (_rms(xp) * final_norm_w).astype(x_prompt.dtype)
    y_sample = (_rms(xs) * final_norm_w).astype(x_sample.dtype)
    hgrn_prompt = jnp.stack(hg_p)
    ssm_prompt = jnp.stack(ssm_p)
    conv_prompt = jnp.stack(conv_p)
    hgrn_sample = jnp.stack(hg_s)
    ssm_sample = jnp.stack(ssm_s)
    conv_sample = jnp.stack(conv_s)
    return (y_prompt, y_sample, hgrn_prompt, ssm_prompt, conv_prompt, hgrn_sample, ssm_sample, conv_sample)
```

```python
import numpy as np
import concourse.bass as bass
import concourse.mybir as mybir
from concourse.bass_utils import run_bass_kernel_spmd

F32 = mybir.dt.float32
BF16 = mybir.dt.bfloat16
AF = mybir.ActivationFunctionType
ALU = mybir.AluOpType
AX = mybir.AxisListType

D = 1024
IN_COLS = 11296
EPS = 1e-6
NEG = -1.0e30


class Tok:
    __slots__ = ("w", "r", "name")

    def __init__(self, name=""):
        self.w = None
        self.r = {}
        self.name = name


class V:
    __slots__ = ("ap", "toks")

    def __init__(self, ap, toks):
        self.ap = ap
        self.toks = toks

    def __getitem__(self, idx):
        return V(self.ap[idx], self.toks)

    def re(self, s, **kw):
        return V(self.ap.rearrange(s, **kw), self.toks)

    def bc(self, shape):
        return V(self.ap.to_broadcast(list(shape)), self.toks)

    def un(self, axis):
        return V(self.ap.unsqueeze(axis), self.toks)

    def pbc(self, n):
        return V(self.ap.partition_broadcast(n), self.toks)


class Prog:
    ENG = ("pe", "act", "dve", "pool", "sp")

    def __init__(self, nc):
        self.nc = nc
        self.q = {e: [] for e in self.ENG}
        self.cnt = {}
        self.sems = {}
        self.waited = {e: {} for e in self.ENG}
        self.ctx = []
        self.n_inst = 0
        for e in self.ENG:
            self._mksem("E_" + e)

    def _mksem(self, key):
        if key not in self.sems:
            cm = self.nc.semaphore(key)
            h = cm.__enter__()
            self.ctx.append(cm)
            self.sems[key] = h
            self.cnt[key] = 0
        return self.sems[key]

    def sb(self, name, shape, dtype=F32):
        cm = self.nc.sbuf_tensor(name, list(shape), dtype)
        t = cm.__enter__()
        self.ctx.append(cm)
        return V(t[tuple(slice(None) for _ in shape)], [Tok(name)])

    def ps(self, name, shape, dtype=F32):
        cm = self.nc.psum_tensor(name, list(shape), dtype)
        t = cm.__enter__()
        self.ctx.append(cm)
        return V(t[tuple(slice(None) for _ in shape)], [Tok(name)])

    def dram(self, name, shape, dtype, kind):
        t = self.nc.dram_tensor(name, list(shape), dtype, kind=kind)
        return V(t.ap(), [Tok(name)])

    def _emit(self, eng, fn, reads, writes, dsem=None, relaxed=False):
        deps = {}

        def add(ev):
            if ev is None:
                return
            k, v = ev
            if deps.get(k, 0) < v:
                deps[k] = v

        for x in reads:
            for t in x.toks:
                add(t.w)
        for x in writes:
            for t in x.toks:
                add(t.w)
                for k, v in t.r.items():
                    add((k, v))
        own = "E_" + eng
        waits = []
        for k, v in deps.items():
            if (eng == "pe" or relaxed) and k == own:
                continue
            if self.waited[eng].get(k, 0) >= v:
                continue
            self.waited[eng][k] = v
            waits.append((self.sems[k], v))
        if dsem is not None:
            self._mksem(dsem)
            self.cnt[dsem] += 16
            ev = (dsem, self.cnt[dsem])
            inc = (self.sems[dsem], 16)
        else:
            self.cnt[own] += 1
            ev = (own, self.cnt[own])
            inc = (self.sems[own], 1)
        for x in reads:
            for t in x.toks:
                if t.r.get(ev[0], 0) < ev[1]:
                    t.r[ev[0]] = ev[1]
        for x in writes:
            for t in x.toks:
                t.w = ev
                t.r = {}
        self.q[eng].append((waits, fn, inc))
        self.n_inst += 1
        return ev

    def mm(self, out, lhsT, rhs, start=True, stop=True):
        def fn(e):
            return e.matmul(out.ap, lhsT.ap, rhs.ap, start=start, stop=stop)
        return self._emit("pe", fn, [lhsT, rhs], [out])

    def tr(self, out, in_, ident):
        def fn(e):
            return e.transpose(out.ap, in_.ap, ident.ap)
        return self._emit("pe", fn, [in_, ident], [out])

    def act(self, out, in_, func, bias=None, scale=None, accum=None):
        rd = [in_]
        kw = {}
        if bias is not None:
            if isinstance(bias, V):
                rd.append(bias)
                kw["bias"] = bias.ap
            else:
                kw["bias"] = float(bias)
        if scale is not None:
            if isinstance(scale, V):
                rd.append(scale)
                kw["scale"] = scale.ap
            else:
                kw["scale"] = float(scale)
        wr = [out]
        if accum is not None:
            wr.append(accum)
            kw["accum_out"] = accum.ap

        def fn(e):
            return e.activation(out.ap, in_.ap, func, **kw)
        return self._emit("act", fn, rd, wr)

    def tt(self, out, in0, in1, op, eng="dve"):
        def fn(e):
            return e.tensor_tensor(out.ap, in0.ap, in1.ap, op)
        return self._emit(eng, fn, [in0, in1], [out])

    def ts(self, out, in0, s1, op0, s2=None, op1=None, eng="dve"):
        rd = [in0]
        a1 = s1.ap if isinstance(s1, V) else float(s1)
        if isinstance(s1, V):
            rd.append(s1)
        a2 = None
        if s2 is not None:
            a2 = s2.ap if isinstance(s2, V) else float(s2)
            if isinstance(s2, V):
                rd.append(s2)

        def fn(e):
            if op1 is None:
                return e.tensor_scalar(out.ap, in0.ap, a1, None, op0)
            return e.tensor_scalar(out.ap, in0.ap, a1, a2, op0, op1)
        return self._emit(eng, fn, rd, [out])

    def stt(self, out, in0, scalar, in1, op0, op1):
        rd = [in0, in1]
        a = scalar.ap if isinstance(scalar, V) else float(scalar)
        if isinstance(scalar, V):
            rd.append(scalar)

        def fn(e):
            return e.scalar_tensor_tensor(out.ap, in0.ap, a, in1.ap, op0, op1)
        return self._emit("dve", fn, rd, [out])

    def copy(self, out, in_, eng="dve"):
        if eng == "act":
            return self.act(out, in_, AF.Copy)

        def fn(e):
            return e.tensor_copy(out.ap, in_.ap)
        return self._emit(eng, fn, [in_], [out])

    def memset(self, out, val, eng="dve"):
        def fn(e):
            return e.memset(out.ap, val)
        return self._emit(eng, fn, [], [out])

    def reduce(self, out, in_, op=None, axis=None):
        op = op or ALU.add
        axis = axis or AX.X

        def fn(e):
            return e.tensor_reduce(out.ap, in_.ap, axis, op)
        return self._emit("dve", fn, [in_], [out])

    def recip(self, out, in_):
        def fn(e):
            return e.reciprocal(out.ap, in_.ap)
        return self._emit("dve", fn, [in_], [out])

    def scan(self, out, d0, d1, initial, op0, op1):
        def fn(e):
            return e.tensor_tensor_scan(out.ap, d0.ap, d1.ap, initial, op0, op1)
        return self._emit("dve", fn, [d0, d1], [out])

    def max8(self, out, in_, relaxed=False):
        def fn(e):
            return e.max(out.ap, in_.ap)
        return self._emit("dve", fn, [in_], [out], relaxed=relaxed)

    def mrep(self, out, rep, vals, imm, relaxed=False):
        def fn(e):
            return e.match_replace(out.ap, rep.ap, vals.ap, imm)
        return self._emit("dve", fn, [rep, vals], [out], relaxed=relaxed)

    def asel(self, out, pattern, cmp, fill, base, cm):
        def fn(e):
            return e.affine_select(out=out.ap, in_=out.ap, pattern=pattern, compare_op=cmp, fill=fill,
                                   base=base, channel_multiplier=cm)
        return self._emit("pool", fn, [out], [out])

    def dma(self, out, in_, dsem, eng="sp", **kw):
        def fn(e):
            return e.dma_start(out=out.ap, in_=in_.ap, **kw)
        return self._emit(eng, fn, [in_], [out], dsem=dsem)

    def finish(self, final_keys):
        nc = self.nc
        engmap = {"pe": "tensor", "act": "scalar", "dve": "vector", "pool": "gpsimd", "sp": "sync"}
        with nc.Block() as block:
            for en in self.ENG:
                ops = self.q[en]
                extra = [(k, self.cnt[k]) for k in final_keys] if en == "sp" else []

                def body(e, ops=ops, extra=extra):
                    for waits, fn, inc in ops:
                        for s, v in waits:
                            e.wait_ge(s, v)
                        fn(e).then_inc(inc[0], inc[1])
                    for k, v in extra:
                        e.wait_ge(self.sems[k], v)
                getattr(block, engmap[en])(body)


def build(NP, TP, TS, n_eblk=16):
    nc = bass.Bass("TRN2", target_bir_lowering=False)
    P = Prog(nc)
    NSEQ = NP + 1
    NE = n_eblk * 1024
    EI, EO = "ExternalInput", "ExternalOutput"
    xp = P.dram("xp", [NP, TP, D], F32, EI)
    xs = P.dram("xs", [1, TS, D], F32, EI)
    cv = P.dram("cv", [NSEQ * 8, 128], F32, EI)
    st_hg = P.dram("st_hg", [8, 128, 128], F32, EI)
    st_ssm = P.dram("st_ssm", [2048, 128], F32, EI)
    st_conv = P.dram("st_conv", [72, 128], F32, EI)
    w_ada = P.dram("w_ada", [D, 6 * D], F32, EI)
    b_ada = P.dram("b_ada", [6 * D], F32, EI)
    norm1_w = P.dram("norm1_w", [8, 128], F32, EI)
    w_in = P.dram("w_in", [D, IN_COLS], F32, EI)
    lbs = P.dram("lbs", [16, 128], F32, EI)
    hgw = P.dram("hgw", [8, 128], F32, EI)
    conv_w = P.dram("conv_w", [96, 128], F32, EI)
    conv_b = P.dram("conv_b", [24, 128], F32, EI)
    dt_bias = P.dram("dt_bias", [32], F32, EI)
    a_log = P.dram("a_log", [32], F32, EI)
    ssm_d = P.dram("ssm_d", [32], F32, EI)
    snw = P.dram("snw", [16, 128], F32, EI)
    w_a = P.dram("w_a", [D, D], F32, EI)
    w_b = P.dram("w_b", [2 * D, D], F32, EI)
    w_o = P.dram("w_o", [D, D], F32, EI)
    norm2_w = P.dram("norm2_w", [8, 128], F32, EI)
    wq = P.dram("wq", [D, 2 * D], F32, EI)
    keysT = P.dram("keysT", [16, 128, 128], F32, EI)
    uT = P.dram("uT", [D, NE], F32, EI)
    pv = P.dram("pv", [NE, D], F32, EI)
    fnw = P.dram("fnw", [D], F32, EI)

    y_p = P.dram("y_p", [NP, TP, D], F32, EO)
    y_s = P.dram("y_s", [1, TS, D], F32, EO)
    hg_p = P.dram("hg_p", [NP, 8, 128, 128], F32, EO)
    ssm_p = P.dram("ssm_p", [NP, 2048, 128], F32, EO)
    conv_p = P.dram("conv_p", [NP, 72, 128], F32, EO)
    hg_s = P.dram("hg_s", [1, 8, 128, 128], F32, EO)
    ssm_s = P.dram("ssm_s", [1, 2048, 128], F32, EO)
    conv_s = P.dram("conv_s", [1, 72, 128], F32, EO)

    w_in_b = P.dram("w_in_b", [D, IN_COLS], BF16, "Internal")
    w_a_b = P.dram("w_a_b", [D, D], BF16, "Internal")
    w_b_b = P.dram("w_b_b", [2 * D, D], BF16, "Internal")
    w_o_b = P.dram("w_o_b", [D, D], BF16, "Internal")
    wq_b = P.dram("wq_b", [D, 2 * D], BF16, "Internal")
    uT_b = P.dram("uT_b", [D, NE], BF16, "Internal")
    pv_b = P.dram("pv_b", [NE, D], BF16, "Internal")

    psd = [P.ps("psd%d" % i, [128, 1024], F32) for i in range(4)]
    ps_out = psd[3]
    ring = {"i": 0}

    def pbank():
        b = psd[ring["i"] % 3]
        ring["i"] += 1
        return b

    ident = P.sb("ident", [128, 128], F32)
    P.memset(ident, 0.0)
    P.asel(ident, [[-1, 128]], ALU.not_equal, 1.0, 0, 1)
    identb = P.sb("identb", [128, 128], BF16)
    P.copy(identb, ident)
    U = P.sb("U", [128, 128], F32)
    P.memset(U, 1.0)
    P.asel(U, [[1, 128]], ALU.is_ge, 0.0, 0, -1)
    ones32 = P.sb("ones32", [32, 128], F32)
    P.memset(ones32, 1.0)
    zerob = P.sb("zerob", [128, 512], BF16)
    P.memset(zerob, 0.0)
    ones128 = P.sb("ones128", [128, 128], F32)
    P.memset(ones128, 1.0)

    scr_all = P.sb("scr_all", [128, 8 * 1024], F32)
    scr = [V(scr_all.ap[:, i * 1024:(i + 1) * 1024], [Tok("scr%d" % i)]) for i in range(8)]
    NB = 12
    bsc_all = P.sb("bsc_all", [128, NB * 1024], BF16)
    bsc = [V(bsc_all.ap[:, i * 1024:(i + 1) * 1024], [Tok("bsc%d" % i)]) for i in range(NB)]

    def loadT(dst, src, R, dsem="ldT"):
        tmp = scr[0]
        P.dma(tmp[:R, 0:128], src, dsem)
        pb = pbank()
        P.tr(pb[:, 0:R], tmp[:R, 0:128], ident[:R, :R])
        P.copy(dst, pb[:, 0:R])

    small = P.sb("small", [128, 512], F32)
    o = 0

    def sm(n):
        nonlocal o
        v = small[:, o:o + n]
        o += n
        return v
    cT = sm(NSEQ * 8)
    lbT2 = sm(16)
    lbT = sm(8)
    omlT = sm(8)
    nomlT = sm(8)
    hgwT = sm(8)
    cwT = sm(96)
    cbT = sm(24)
    snwT = sm(16)
    n1T = sm(8)
    n2T = sm(8)
    scT = sm(NSEQ * 8)
    badaT = sm(48)
    dtb_b = sm(32)
    aneg_b = sm(32)
    dsk_b = sm(32)
    oh32 = sm(32)
    modT = P.sb("modT", [128, 48, NSEQ], F32)
    A1T = P.sb("A1T", [128, NSEQ, 8], F32)
    A2T = P.sb("A2T", [128, NSEQ, 8], F32)
    stcT = P.sb("stcT", [128, 72], F32)

    loadT(cT, cv, NSEQ * 8)
    loadT(lbT2, lbs, 16)
    loadT(hgwT, hgw, 8)
    loadT(cwT, conv_w, 96)
    loadT(cbT, conv_b, 24)
    loadT(snwT, snw, 16)
    loadT(n1T, norm1_w, 8)
    loadT(n2T, norm2_w, 8)
    loadT(badaT, b_ada.re("(a p) -> a p", p=128), 48)
    loadT(stcT, st_conv, 72)
    P.copy(oh32, ident[:, 0:32])
    P.tt(lbT, lbT2[:, 0:8], lbT2[:, 8:16], ALU.subtract)
    P.act(lbT, lbT, AF.Sigmoid)
    P.ts(omlT, lbT, -1.0, ALU.mult, 1.0, ALU.add)
    P.ts(nomlT, omlT, -1.0, ALU.mult)
    P.dma(dtb_b, dt_bias.pbc(128), "ld_a")
    P.dma(aneg_b, a_log.pbc(128), "ld_b")
    P.dma(dsk_b, ssm_d.pbc(128), "ld_c")
    P.act(aneg_b, aneg_b, AF.Exp)
    P.ts(aneg_b, aneg_b, -1.0, ALU.mult)
    fnw_b = P.sb("fnw_b", [128, D], F32)
    P.dma(fnw_b, fnw.pbc(128), "ld_d")
    keysb = P.sb("keysb", [128, 16, 128], BF16)
    for a in range(2):
        P.dma(scr[1].re("p (a n) -> p a n", a=8), keysT[a * 8:(a + 1) * 8].re("a d n -> d a n"), "ld_e")
        P.copy(keysb[:, a * 8:(a + 1) * 8, :], scr[1].re("p (a n) -> p a n", a=8))
    def cvt(dst, src, rows, step, key):
        for r0 in range(0, rows, step):
            P.dma(dst[r0:r0 + step], src[r0:r0 + step], key, eng="pool")
        dst.toks[0].w = (key, P.cnt[key])
    cvt(w_in_b, w_in, D, 128, "cv_in")
    cvt(w_a_b, w_a, D, 256, "cv_a")
    cvt(w_b_b, w_b, 2 * D, 256, "cv_b")
    cvt(w_o_b, w_o, D, 256, "cv_o")
    cvt(wq_b, wq, D, 256, "cv_q")
    cvt(uT_b, uT, D, 64, "cv_u")
    cvt(pv_b, pv, NE, 256, "cv_v")

    P.act(scT, cT, AF.Silu)

    NSLOT = 4
    wslots = [P.sb("wslot%d" % i, [128, 8, 1024], BF16) for i in range(NSLOT)]
    wr = {"i": 0}

    def wload(src_rows_cols, ncols, dsem_extra=""):
        i = wr["i"] % NSLOT
        wr["i"] += 1
        s = wslots[i]
        P.dma(s[:, :, 0:ncols], src_rows_cols.re("(k p) c -> p k c", p=128), "ws%d" % i)
        return s

    ada_f = [V(wslots[i].ap.bitcast(F32), wslots[i].toks) for i in range(2)]

    def ada_load(i, c0):
        s = ada_f[i % 2]
        P.dma(s, w_ada[:, c0:c0 + 512].re("(k p) c -> p k c", p=128), "ada%d" % (i % 2))
        return s
    fm_tiles = list(range(0, 16)) + list(range(24, 40))
    li = 0
    for b0 in (0, 512, 1024, 1536, 3072, 3584, 4096, 4608):
        s = ada_load(li, b0)
        li += 1
        pb = pbank()
        for j in range(4):
            tile = b0 // 128 + j
            for k in range(8):
                P.mm(pb[:, j * NSEQ:(j + 1) * NSEQ], s[:, k, j * 128:(j + 1) * 128],
                     scT.re("p (s k) -> p k s", k=8)[:, k, :], start=(k == 0), stop=(k == 7))
            P.tt(modT[:, tile, :], pb[:, j * NSEQ:(j + 1) * NSEQ], badaT[:, tile:tile + 1].bc([128, NSEQ]), ALU.add)
    for sq in range(NSEQ):
        P.stt(A1T[:, sq, :], modT[:, 8:16, sq], 1.0, n1T, ALU.add, ALU.mult)
        P.stt(A2T[:, sq, :], modT[:, 32:40, sq], 1.0, n2T, ALU.add, ALU.mult)

    g1b = P.sb("g1b", [128, D], F32)
    g2b = P.sb("g2b", [128, D], F32)
    S = P.sb("S", [128, 8, 128], F32)
    H = P.sb("H", [128, 2048], F32)
    Hb = P.sb("Hb", [128, 2048], BF16)
    xb = P.sb("xb", [128, 24, 131], F32)
    xt_ring = [P.sb("xt%d" % i, [128, D], F32) for i in range(2)]
    scb = scr[5].re("p (k t) -> p k t", k=8)
    stat = P.sb("stat", [128, 64], F32)
    hT = P.sb("hT", [128, 8, 128], BF16)
    h2T = P.sb("h2T", [128, 8, 128], BF16)
    xcs = P.sb("xcs", [128, 24, 128], BF16)
    ctmp = [P.sb("ctmp%d" % i, [128, 128], F32) for i in range(2)]
    ctmp2 = [P.sb("ctmpb%d" % i, [128, 128], F32) for i in range(2)]
    junk = P.sb("junk", [128, 1024], BF16)
    att_m = [P.sb("attm%d" % i, [128, 128], BF16) for i in range(2)]
    Smb = [P.sb("Smb%d" % i, [128, 128], BF16) for i in range(2)]
    tmpS = [P.sb("tmpS%d" % i, [128, 128], F32) for i in range(2)]
    gst = P.sb("gst", [128, 40], F32)
    dts = P.sb("dts", [128, 6, 32], F32)
    acT = P.sb("acT", [32, 128], F32)
    Xgf = V(scr[0].ap[:32, :], scr[0].toks)
    negm = {}
    for T_ in sorted(set([128, TS])):
        nm_ = P.sb("negm%d" % T_, [128, 4 * T_], F32)
        for r_ in range(4):
            P.ts(nm_[:, r_ * T_:(r_ + 1) * T_], U[:, 0:T_], -1.0, ALU.add, 1.0e30, ALU.mult)
        negm[T_] = nm_
    eal = P.sb("eal", [128, 32], F32)
    dgl = P.sb("dgl", [32, 32], F32)
    s12 = V(scr_all.ap[:, 6 * 1024:8 * 1024].rearrange("p (a n) -> p a n", a=16), scr[6].toks + scr[7].toks)
    t16 = P.sb("t16", [128, 16, 16], F32)
    wk = P.sb("wk", [128, 128], F32)
    cand = V(scr_all.ap[:, 4 * 1024:6 * 1024].rearrange("p (h c) -> p h c", h=8), scr[4].toks + scr[5].toks)
    cwk = P.sb("cwk", [128, 256], F32)
    c16 = P.sb("c16", [128, 8, 16], F32)
    pst = P.sb("pst", [128, 8, 8], F32)
    qT = V(bsc_all.ap[:, 9 * 1024:11 * 1024].rearrange("p (a t) -> p a t", a=16), bsc[9].toks + bsc[10].toks)

    def rms_rstd(dst, ss, n):
        P.act(dst, ss, AF.Sqrt, bias=EPS, scale=1.0 / n)
        P.recip(dst, dst)

    xin_cnt = {"i": 0}

    def norm_to_T(xsrc, T, AT, shT, dstT):
        ss = stat[:T, 0:1]
        P.act(junk[:T, :], xsrc[:T, :], AF.Square, accum=ss)
        rs = stat[:T, 1:2]
        rms_rstd(rs, ss, D)
        xn = scr[7]
        P.ts(xn[:T, :], xsrc[:T, :], rs, ALU.mult)
        pb = pbank()
        for k in range(8):
            P.tr(pb[:, k * T:(k + 1) * T], xn[:T, k * 128:(k + 1) * 128], ident[:T, :T])
        for k in range(8):
            P.act(dstT[:, k, :T], pb[:, k * T:(k + 1) * T], AF.Identity, bias=shT[:, k:k + 1], scale=AT[:, k:k + 1])

    def fm_block_T(c0, T, ntile=8):
        s = wload(w_in_b[:, c0:c0 + ntile * 128], ntile * 128)
        pb = pbank()
        for j in range(ntile):
            for k in range(8):
                P.mm(pb[:, j * T:(j + 1) * T], s[:, k, j * 128:(j + 1) * 128], hT[:, k, :T],
                     start=(k == 0), stop=(k == 7))
        return pb

    def front(sq, xsrc_dram, T):
        xt = xt_ring[xin_cnt["i"] % 2]
        xin_cnt["i"] += 1
        P.dma(xt[:T, :], xsrc_dram, "xin%d" % (xin_cnt["i"] % 2))
        norm_to_T(xt, T, A1T[:, sq, :], modT[:, 0:8, sq], hT)
        pb = fm_block_T(0, T)
        P.act(scr[0][:, 0:8 * T], pb[:, 0:8 * T], AF.Silu)
        pb = fm_block_T(1024, T)
        P.act(scr[1][:, 0:8 * T], pb[:, 0:8 * T], AF.Sigmoid)
        return xt

    def chunk(sq, xt, ydst_dram, T, hook):
        hm = T // 2 - 1

        def fm_block(c0, ntile=8):
            return fm_block_T(c0, T, ntile)

        def tm_block(wsrc, c0, ncols, lhs):
            s = wload(wsrc[:, c0:c0 + ncols], ncols)
            pb = pbank()
            for h0 in range(0, ncols, 512):
                w = min(512, ncols - h0)
                for k in range(8):
                    P.mm(pb[:T, h0:h0 + w], lhs[:, k, :T], s[:, k, h0:h0 + w], start=(k == 0), stop=(k == 7))
            return pb

        qs, sg, lf, g, dd, eq, ek = scr[0], scr[1], scr[2], scr[3], scr[4], scr[5], scr[6]
        v3 = lambda x: V(x.ap[:, 0:8 * T].rearrange("p (h t) -> p h t", h=8), x.toks)
        f2 = lambda x: x[:, 0:8 * T]
        for h in range(8):
            P.act(v3(lf)[:, h, :], v3(sg)[:, h, :], AF.Ln, bias=lbT[:, h:h + 1], scale=omlT[:, h:h + 1])
        for h in range(8):
            P.ts(v3(sg)[:, h, :], v3(sg)[:, h, :], nomlT[:, h:h + 1], ALU.mult, omlT[:, h:h + 1], ALU.add)
        for h in range(8):
            P.scan(v3(g)[:, h, :], ones128[:, :T], v3(lf)[:, h, :], 0.0, ALU.mult, ALU.add)
        gm = gst[:, 0:8]
        gl = gst[:, 8:16]
        P.copy(gm, v3(g)[:, :, hm])
        P.copy(gl, v3(g)[:, :, T - 1])
        P.tt(v3(dd), v3(g), gm.un(2).bc([128, 8, T]), ALU.subtract)
        P.act(f2(eq), f2(dd), AF.Exp)
        P.act(f2(ek), f2(dd), AF.Exp, scale=-1.0)
        egm, eL, eLM = gst[:, 16:24], gst[:, 24:32], gst[:, 32:40]
        P.act(egm, gm, AF.Exp)
        P.act(eL, gl, AF.Exp)
        P.copy(eLM, v3(eq)[:, :, T - 1])
        QT, KT, Kt = bsc[0], bsc[1], bsc[2]
        P.tt(f2(QT), f2(qs), f2(eq), ALU.mult)
        P.tt(f2(KT), f2(sg), f2(ek), ALU.mult)
        vv, sgl, sz0, sz1, sga, sgb = bsc[3], bsc[4], bsc[5], bsc[6], bsc[7], bsc[8]
        pb = tm_block(w_in_b, 2048, 1024, hT)
        P.copy(vv[:T, :], pb[:T, :], eng="act")
        pb = tm_block(w_in_b, 3072, 1024, hT)
        P.act(sgl[:T, :], pb[:T, :], AF.Silu)
        pb = tm_block(w_in_b, 9248, 1024, hT)
        P.act(sga[:T, :], pb[:T, :], AF.Sigmoid)

        pbk = pbank()
        pbkb = V(pbk.ap.bitcast(BF16), pbk.toks)
        for h in range(8):
            P.tr(pbkb[:T, h * 128:(h + 1) * 128], v3(KT)[:, h, :], identb)
        P.copy(Kt[:T, :], pbkb[:T, 0:1024], eng="act")
        po = pbank()
        pu = pbank()
        patt = pbank()
        for h in range(8):
            P.mm(patt[:T, h * 128:h * 128 + T], v3(KT)[:, h, :], v3(QT)[:, h, :])
        am_all, smb_all = bsc[5], bsc[6]
        am3 = V(am_all.ap[:T, 0:8 * T].rearrange("p (h i) -> p h i", h=8), am_all.toks)
        P.tt(am3, patt[:T, :].re("p (h c) -> p h c", h=8)[:, :, 0:T], U[:T, :T].un(1).bc([T, 8, T]), ALU.mult)
        P.tt(smb_all.re("p (h v) -> p h v", h=8), S, egm.un(2).bc([128, 8, 128]), ALU.mult)
        for h in range(8):
            oc = slice(h * 128, (h + 1) * 128)
            P.mm(po[:T, oc], am3[:, h, :], vv[:T, oc], start=True, stop=False)
            P.mm(po[:T, oc], v3(QT)[:, h, :], smb_all[:, oc], start=False, stop=True)
        for h in range(8):
            oc = slice(h * 128, (h + 1) * 128)
            P.mm(pu[:, oc], Kt[:T, oc], vv[:T, oc])
        tsa = scr[2]
        P.tt(tsa.re("p (h v) -> p h v", h=8), pu.re("p (h v) -> p h v", h=8), eLM.un(2).bc([128, 8, 128]), ALU.mult)
        P.tt(S, S, eL.un(2).bc([128, 8, 128]), ALU.mult)
        P.tt(S, S, tsa.re("p (h v) -> p h v", h=8), ALU.add)
        for h in range(8):
            P.act(junk[:T, 0:128], po[:T, h * 128:(h + 1) * 128], AF.Square, accum=stat[:T, 8 + h:9 + h])
        rs8 = stat[:T, 16:24]
        rms_rstd(rs8, stat[:T, 8:16], 128)
        on = scr[0]
        P.tt(on[:T, :].re("p (h v) -> p h v", h=8), po[:T, :].re("p (h v) -> p h v", h=8),
             rs8.un(2).bc([T, 8, 128]), ALU.mult)
        P.tt(on[:T, :], on[:T, :], sgl[:T, :], ALU.mult)
        pb = pbank()
        for k in range(8):
            P.tr(pb[:, k * T:(k + 1) * T], on[:T, k * 128:(k + 1) * 128], ident[:T, :T])
        obT = bsc[0]
        for k in range(8):
            P.act(obT[:, k * T:(k + 1) * T], pb[:, k * T:(k + 1) * T], AF.Identity, scale=hgwT[:, k:k + 1])
        obT3 = V(obT.ap[:, 0:8 * T].rearrange("p (k t) -> p k t", k=8), obT.toks)
        ppa = tm_block(w_a_b, 0, 1024, obT3)
        t1 = scr[1]
        P.tt(t1[:T, :], ppa[:T, :], sga[:T, :], ALU.mult)

        for b in range(3):
            pb = fm_block(6144 + b * 1024)
            P.copy(xb[:, b * 8:(b + 1) * 8, 3:3 + T], pb[:, 0:8 * T].re("p (a t) -> p a t", a=8), eng="act")
            bs = slice(b * 8, (b + 1) * 8)
            cacc = V(scr[3].ap[:, 0:8 * T].rearrange("p (a t) -> p a t", a=8), scr[3].toks)
            ctm = V(scr[4].ap[:, 0:8 * T].rearrange("p (a t) -> p a t", a=8), scr[4].toks)
            P.tt(cacc, xb[:, bs, 0:T], cwT[:, b * 8:(b + 1) * 8].un(2).bc([128, 8, T]), ALU.mult)
            for j in range(1, 4):
                P.tt(ctm, xb[:, bs, j:j + T], cwT[:, j * 24 + b * 8:j * 24 + (b + 1) * 8].un(2).bc([128, 8, T]), ALU.mult)
                P.tt(cacc, cacc, ctm, ALU.add)
            P.tt(cacc, cacc, cbT[:, b * 8:(b + 1) * 8].un(2).bc([128, 8, T]), ALU.add)
            P.act(xcs[:, bs, :T], cacc, AF.Silu)
        pb = tm_block(w_in_b, 4096, 1024, hT)
        P.act(sz0[:T, :], pb[:T, :], AF.Silu)
        pb = tm_block(w_in_b, 5120, 1024, hT)
        P.act(sz1[:T, :], pb[:T, :], AF.Silu)
        s = wload(w_in_b[:, 9216:9248], 32)
        pb = pbank()
        for k in range(8):
            P.mm(pb[:T, 0:32], hT[:, k, :T], s[:, k, 0:32], start=(k == 0), stop=(k == 7))
        dtv, dav, acv, eac = dts[:, 0, :], dts[:, 1, :], dts[:, 2, :], dts[:, 3, :]
        P.tt(dtv[:T], pb[:T, 0:32], dtb_b[:T], ALU.add)
        P.act(dtv[:T], dtv[:T], AF.Exp)
        P.act(dtv[:T], dtv[:T], AF.Ln, bias=1.0)
        P.tt(dav[:T], dtv[:T], aneg_b[:T], ALU.mult)
        pb = tm_block(w_in_b, 10272, 1024, hT)
        P.act(sgb[:T, :], pb[:T, :], AF.Sigmoid)


        pb = pbank()
        P.mm(pb[:T, 0:32], U[:T, :T], dav[:T])
        P.mm(pb[:32, 512:512 + T], dav[:T], U[:T, :T])
        P.copy(acv[:T], pb[:T, 0:32])
        P.copy(acT[:, :T], pb[:32, 512:512 + T])
        P.act(eac[:T], acv[:T], AF.Exp)
        P.ts(dgl, oh32[:32, :], acT[:, T - 1:T], ALU.mult)
        P.mm(pb[:, 256:288], ones32[:, :], dgl)
        P.act(eal, pb[:, 256:288], AF.Exp)
        Btm = bsc[2]
        pbx = pbank()
        pbxb = V(pbx.ap.bitcast(BF16), pbx.toks)
        for tl in range(16):
            P.tr(pbxb[:T, tl * 128:(tl + 1) * 128], xcs[:, tl, :T], identb)
        xtmA = scr[2]
        xtm_v = V(xtmA.ap.bitcast(BF16), xtmA.toks)
        P.copy(xtm_v[:T, :], pbxb[:T, :], eng="act")
        pbB = pbank()
        pbBb = V(pbB.ap.bitcast(BF16), pbB.toks)
        for gq in range(4):
            P.tr(pbBb[:T, gq * 128:(gq + 1) * 128], xcs[:, 16 + gq, :T], identb)
        P.copy(Btm[:T, 0:512], pbBb[:T, 0:512], eng="act")
        xdt_v = V(scr[3].ap.bitcast(BF16), scr[3].toks)
        xD_v = V(scr[4].ap.bitcast(BF16), scr[4].toks)
        r3 = lambda x: x[:T, :].re("p (r q) -> p r q", r=32)
        P.tt(r3(xdt_v), r3(xtm_v), dtv[:T].un(2).bc([T, 32, 64]), ALU.mult)
        P.tt(r3(xD_v), r3(xtm_v), dsk_b[:T].un(2).bc([T, 32, 64]), ALU.mult)
        yb = [scr[5], scr[6]]
        wT_r = [bsc[11], bsc[9]]
        xw_r = [bsc[1], bsc[10]]
        Dm = scr[7]

        def stageA(gq):
            BT = xcs[:, 16 + gq, :T]
            CT = xcs[:, 20 + gq, :T]
            pc = pbank()
            P.mm(pc[:T, 0:T], BT, CT)
            cbmm = tmpS[gq % 2]
            P.tt(cbmm[:T, :T], pc[:T, 0:T], U[:T, :T], ALU.mult)
            P.tt(V(Xgf.ap[:, 0:8 * T].rearrange("p (r i) -> p r i", r=8), Xgf.toks), acT[:, :T].un(1).bc([32, 8, T]),
                 oh32[:32, gq * 8:(gq + 1) * 8].un(2).bc([32, 8, T]), ALU.mult)
            pd_ = pbank()
            for hf in range(2):
                osl = slice(hf * 4 * T, (hf + 1) * 4 * T)
                P.mm(pd_[:T, osl], ones32[:, :T], Xgf[:, osl], start=True, stop=False)
                P.mm(pd_[:T, osl], ident[:T, :T], negm[T][:T, :], start=False, stop=True)
            P.tt(V(Dm.ap[:T, 0:8 * T].rearrange("p (r i) -> p r i", r=8), Dm.toks),
                 pd_[:T, 0:8 * T].re("p (r i) -> p r i", r=8),
                 acv[:T, gq * 8:(gq + 1) * 8].un(2).bc([T, 8, T]), ALU.subtract)
            P.act(Dm[:T, 0:8 * T], Dm[:T, 0:8 * T], AF.Exp)
            E3 = V(Dm.ap[:T, 0:8 * T].rearrange("p (r i) -> p r i", r=8), Dm.toks)
            wT = wT_r[gq % 2]
            wT3 = V(wT.ap[:T, 0:8 * T].rearrange("p (r i) -> p r i", r=8), wT.toks)
            P.tt(wT3, E3, cbmm[:T, :T].un(1).bc([T, 8, T]), ALU.mult)
            xw = xw_r[gq % 2]
            gsl = slice(gq * 512, (gq + 1) * 512)
            P.tt(xw[:T, 0:512].re("p (r q) -> p r q", r=8), xdt_v[:T, gsl].re("p (r q) -> p r q", r=8),
                 E3[:, :, T - 1:T].bc([T, 8, 64]), ALU.mult)

        def stageB(gq):
            CT = xcs[:, 20 + gq, :T]
            wT = wT_r[gq % 2]
            wT3 = V(wT.ap[:T, 0:8 * T].rearrange("p (r i) -> p r i", r=8), wT.toks)
            xw = xw_r[gq % 2]
            gsl = slice(gq * 512, (gq + 1) * 512)
            pin = pbank()
            P.mm(pin[:T, 0:512], identb[:T, :T], xD_v[:T, gsl], start=True, stop=False)
            for r in range(8):
                P.mm(pin[:T, r * 64:(r + 1) * 64], wT3[:, r, :], xdt_v[:T, gq * 512 + r * 64:gq * 512 + (r + 1) * 64],
                     start=False, stop=(r == 7))
            P.mm(pin[:T, 512:1024], CT, Hb[:, gsl])
            yg = yb[gq // 2][:T, (gq % 2) * 512:(gq % 2 + 1) * 512]
            P.tt(yg.re("p (r q) -> p r q", r=8), pin[:T, 512:1024].re("p (r q) -> p r q", r=8),
                 eac[:T, gq * 8:(gq + 1) * 8].un(2).bc([T, 8, 64]), ALU.mult)
            P.tt(yg, yg, pin[:T, 0:512], ALU.add)
            szg = (sz0 if gq < 2 else sz1)[:T, (gq % 2) * 512:(gq % 2 + 1) * 512]
            P.tt(yg, yg, szg, ALU.mult)
            P.act(junk[:T, 0:512], yg, AF.Square, accum=stat[:T, 24 + gq:25 + gq])
            pup = pbank()
            P.mm(pup[:, 0:512], Btm[:T, gq * 128:(gq + 1) * 128], xw[:T, 0:512])
            P.tt(H[:, gsl].re("p (r q) -> p r q", r=8), H[:, gsl].re("p (r q) -> p r q", r=8),
                 eal[:, gq * 8:(gq + 1) * 8].un(2).bc([128, 8, 64]), ALU.mult)
            P.tt(H[:, gsl], H[:, gsl], pup[:, 0:512], ALU.add)
            P.copy(Hb[:, gsl], H[:, gsl], eng="act")

        stageA(0)
        stageA(1)
        stageB(0)
        stageA(2)
        stageB(1)
        stageA(3)
        stageB(2)
        stageB(3)
        rs4 = stat[:T, 28:32]
        rms_rstd(rs4, stat[:T, 24:28], 512)
        for half in range(2):
            yh = yb[half]
            P.tt(yh[:T, :].re("p (a q) -> p a q", a=2), yh[:T, :].re("p (a q) -> p a q", a=2),
                 rs4[:, half * 2:half * 2 + 2].un(2).bc([T, 2, 512]), ALU.mult)
        ynT = [bsc[0], bsc[2]]
        for half in range(2):
            pb = pbank()
            for k in range(8):
                P.tr(pb[:, k * T:(k + 1) * T], yb[half][:T, k * 128:(k + 1) * 128], ident[:T, :T])
            for k in range(8):
                P.act(ynT[half][:, k * T:(k + 1) * T], pb[:, k * T:(k + 1) * T], AF.Identity,
                      scale=snwT[:, half * 8 + k:half * 8 + k + 1])
        ppb = pbank()
        for half in range(2):
            s = wload(w_b_b[half * 1024:(half + 1) * 1024, :], 1024)
            yv = V(ynT[half].ap[:, 0:8 * T].rearrange("p (k t) -> p k t", k=8), ynT[half].toks)
            for h0 in (0, 512):
                for k in range(8):
                    P.mm(ppb[:T, h0:h0 + 512], yv[:, k, :], s[:, k, h0:h0 + 512], start=(half == 0 and k == 0),
                         stop=(half == 1 and k == 7))
        t2 = scr[2]
        P.tt(t2[:T, :], ppb[:T, :], sgb[:T, :], ALU.mult)
        P.tt(t1[:T, :], t1[:T, :], t2[:T, :], ALU.add)
        pb = pbank()
        for k in range(8):
            P.tr(pb[:, k * T:(k + 1) * T], t1[:T, k * 128:(k + 1) * 128], ident[:T, :T])
        mT = bsc[1]
        P.copy(mT[:, 0:8 * T], pb[:, 0:8 * T], eng="act")
        mT3 = V(mT.ap[:, 0:8 * T].rearrange("p (k t) -> p k t", k=8), mT.toks)
        pmix = tm_block(w_o_b, 0, 1024, mT3)
        P.tt(t2[:T, :], pmix[:T, :], g1b[:T, :], ALU.mult)
        P.tt(xt[:T, :], xt[:T, :], t2[:T, :], ALU.add)
        P.copy(ctmp[0][:, 0:72].re("p (a j) -> p a j", a=24), xb[:, :, T:T + 3])
        P.copy(xb[:, :, 0:3], ctmp[0][:, 0:72].re("p (a j) -> p a j", a=24))

        norm_to_T(xt, T, A2T[:, sq, :], modT[:, 24:32, sq], h2T)
        for b in range(2):
            s = wload(wq_b[:, b * 1024:(b + 1) * 1024], 1024)
            pb = pbank()
            for j in range(8):
                for k in range(8):
                    P.mm(pb[:, j * T:(j + 1) * T], s[:, k, j * 128:(j + 1) * 128], h2T[:, k, :T],
                         start=(k == 0), stop=(k == 7))
            P.copy(qT[:, b * 8:(b + 1) * 8, :T], pb[:, 0:8 * T].re("p (a t) -> p a t", a=8), eng="act")
        for b in range(2):
            pb = pbank()
            for j in range(8):
                ct = b * 8 + j
                hh, half = ct // 2, ct % 2
                P.mm(pb[:T, j * 128:(j + 1) * 128], qT[:, ct, :T], keysb[:, half * 8 + hh, :])
            P.copy(s12[:T, b * 8:(b + 1) * 8, :], pb[:T, :].re("p (a n) -> p a n", a=8), eng="act")
        wk16 = V(scr_all.ap[:, 0:2048].rearrange("p (a n) -> p a n", a=16), scr[0].toks + scr[1].toks)
        for ct in range(16):
            P.max8(t16[:T, ct, 0:8], s12[:T, ct, :], relaxed=(ct > 0))
        for ct in range(16):
            P.mrep(wk16[:T, ct, :], t16[:T, ct, 0:8], s12[:T, ct, :], NEG, relaxed=(ct > 0))
        for ct in range(16):
            P.max8(t16[:T, ct, 8:16], wk16[:T, ct, :], relaxed=(ct > 0))
        t4 = t16[:T].re("p (h a) k -> p h a k", a=2)
        for h in range(8):
            P.tt(cand[:T, h, :].re("p (a b) -> p a b", a=16), t4[:, h, 0, :].un(2).bc([T, 16, 16]),
                 t4[:, h, 1, :].un(1).bc([T, 16, 16]), ALU.add)
        cwk8 = V(scr_all.ap[:, 2048:4096].rearrange("p (h c) -> p h c", h=8), scr[2].toks + scr[3].toks)
        for h in range(8):
            P.max8(c16[:T, h, 0:8], cand[:T, h, :], relaxed=(h > 0))
        for h in range(8):
            P.mrep(cwk8[:T, h, :], c16[:T, h, 0:8], cand[:T, h, :], NEG, relaxed=(h > 0))
        for h in range(8):
            P.max8(c16[:T, h, 8:16], cwk8[:T, h, :], relaxed=(h > 0))
        tau = pst[:T, :, 0]
        mx = pst[:T, :, 1]
        Zs = pst[:T, :, 2]
        nb = pst[:T, :, 3]
        P.copy(tau, c16[:T, :, 15])
        P.copy(mx, c16[:T, :, 0])
        e16 = cwk[:T, 0:128].re("p (h k) -> p h k", h=8)
        P.tt(e16, c16[:T], mx.un(2).bc([T, 8, 16]), ALU.subtract)
        P.act(e16, e16, AF.Exp)
        P.reduce(Zs, e16)
        P.act(Zs, Zs, AF.Ln)
        P.tt(nb, mx, Zs, ALU.add)
        P.ts(nb, nb, -1.0, ALU.mult)
        xd_r = [scr[0], scr[1], scr[3], scr[4]]
        POOL_H = ()
        ex_r = [bsc[0], bsc[1], bsc[11]]
        Wh_r = [bsc[2], bsc[3], bsc[4]]
        gA_r = [bsc[5], bsc[6]]
        GT_r = [bsc[7], bsc[8]]
        pw_r = [psd[0], psd[1]]
        pa = psd[2]
        NS = n_eblk * 8
        wsl = {}

        def L(eb):
            su = wload(uT_b[:, eb * 1024:(eb + 1) * 1024], 1024)
            i = wr["i"] % NSLOT
            wr["i"] += 1
            sv = wslots[i]
            P.dma(sv, pv_b[eb * 1024:(eb + 1) * 1024, :].re("(a p) c -> p a c", p=128), "ws%d" % i)
            wsl[eb] = (su, sv)

        def P1(s_):
            eb, h = divmod(s_, 8)
            xd = xd_r[s_ % 4]
            P.tt(xd[:T, :].re("p (i j) -> p i j", i=8), s12[:T, 2 * h, eb * 8:(eb + 1) * 8].un(2).bc([T, 8, 128]),
                 s12[:T, 2 * h + 1, :].un(1).bc([T, 8, 128]), ALU.add, eng=("pool" if h in POOL_H else "dve"))
            P.act(ex_r[s_ % 3][:T, :], xd[:T, :], AF.Exp, bias=pst[:T, h, 3:4])

        def P3(s_):
            eb, h = divmod(s_, 8)
            xd, ex, Wh = xd_r[s_ % 4], ex_r[s_ % 3], Wh_r[s_ % 3]
            P.stt(Wh[:T, :], xd[:T, :], pst[:T, h, 0:1], ex[:T, :], ALU.is_ge, ALU.mult)
            pw = pw_r[eb % 2]
            if h == 0:
                for h0 in (0, 512):
                    P.mm(pw[:, h0:h0 + 512], zerob[:, 0:128], zerob[:, 0:512], start=True, stop=False)
            for a in range(8):
                P.mm(pw[:, a * T:(a + 1) * T], Wh[:T, a * 128:(a + 1) * 128], identb[:T, :T], start=False,
                     stop=(h == 7 and (a == 7 or (a + 1) * T % 512 == 0)))

        def Atile(eb, a):
            su, sv = wsl[eb]
            for k in range(8):
                P.mm(pa[:, a * T:(a + 1) * T], su[:, k, a * 128:(a + 1) * 128], h2T[:, k, :T], start=(k == 0), stop=(k == 7))
            if a == 7:
                P.act(gA_r[eb % 2][:, 0:8 * T], pa[:, 0:8 * T], AF.Gelu)

        def Gstage(eb):
            P.tt(GT_r[eb % 2][:, 0:8 * T], pw_r[eb % 2][:, 0:8 * T], gA_r[eb % 2][:, 0:8 * T], ALU.mult)

        def Ostage(eb, alist):
            su, sv = wsl[eb]
            GT = GT_r[eb % 2]
            for a in alist:
                for h0 in (0, 512):
                    P.mm(ps_out[:T, h0:h0 + 512], GT[:, a * T:(a + 1) * T], sv[:, a, h0:h0 + 512],
                         start=(eb == 0 and a == 0), stop=(eb == n_eblk - 1 and a == 7))

        L(0)
        if n_eblk > 1:
            L(1)
        P1(0)
        P1(1)
        for s_ in range(NS):
            eb, h = divmod(s_, 8)
            if s_ + 2 < NS:
                P1(s_ + 2)
            P3(s_)
            Atile(eb, h)
            if h == 3 and eb >= 1:
                Gstage(eb - 1)
            if h >= 4 and eb >= 1:
                Ostage(eb - 1, [2 * (h - 4), 2 * (h - 4) + 1])
                if h == 7 and eb + 1 < n_eblk:
                    L(eb + 1)
        Gstage(n_eblk - 1)
        nxt = hook() if hook is not None else None
        Ostage(n_eblk - 1, list(range(8)))
        t2 = scr[2]
        P.tt(t2[:T, :], ps_out[:T, :], g2b[:T, :], ALU.mult)
        P.tt(xt[:T, :], xt[:T, :], t2[:T, :], ALU.add)
        P.act(junk[:T, :], xt[:T, :], AF.Square, accum=stat[:T, 32:33])
        rms_rstd(stat[:T, 33:34], stat[:T, 32:33], D)
        yo = scr[5]
        P.stt(yo[:T, :], xt[:T, :], stat[:T, 33:34], fnw_b[:T, :], ALU.mult, ALU.mult)
        P.dma(ydst_dram, yo[:T, :], "yout", eng="pool")
        return nxt

    seqs = [(i, "p") for i in range(NP)] + [(0, "s")]
    for sq, (bi, kind) in enumerate(seqs):
        T = 128 if kind == "p" else TS
        nch = TP // 128 if kind == "p" else 1
        P.copy(scb, scT[:, sq * 8:(sq + 1) * 8].un(2).bc([128, 8, 128]))
        for gi, (dst, c0) in enumerate(((g1b, 2048), (g2b, 5120))):
            for hb in range(2):
                s = ada_load(li, c0 + hb * 512)
                li += 1
                bb = scr[6]
                P.dma(bb[:, 0:512], b_ada[c0 + hb * 512:c0 + (hb + 1) * 512].pbc(128), "ld_bb")
                pb = pbank()
                for k in range(8):
                    P.mm(pb[:, 0:512], scb[:, k, :], s[:, k, :], start=(k == 0), stop=(k == 7))
                P.tt(dst[:, hb * 512:(hb + 1) * 512], pb[:, 0:512], bb[:, 0:512], ALU.add)
        if kind == "p":
            P.memset(S, 0.0)
            P.memset(H, 0.0)
            P.memset(Hb, 0.0)
            P.memset(xb[:, :, 0:3], 0.0)
        else:
            P.dma(S, st_hg.re("h k v -> k h v"), "ld_S")
            for half in range(2):
                tmp = scr[0]
                P.dma(tmp.re("p (a n) -> p a n", a=8), st_ssm[half * 1024:(half + 1) * 1024, :].re("(a p) n -> p a n", p=128), "ld_ssm")
                pb = pbank()
                for a in range(8):
                    P.tr(pb[:, a * 128:(a + 1) * 128], tmp[:, a * 128:(a + 1) * 128], ident)
                P.copy(H[:, half * 1024:(half + 1) * 1024], pb)
            P.copy(Hb, H)
            P.copy(xb[:, :, 0:3], stcT.re("p (j a) -> p a j", j=3))
        if kind == "p":
            xsrc = lambda c, bi=bi: xp[bi, c * 128:(c + 1) * 128, :]
            ydst = lambda c, bi=bi: y_p[bi, c * 128:(c + 1) * 128, :]
        else:
            xsrc = lambda c: xs[0, :, :]
            ydst = lambda c: y_s[0, :, :]
        xt_cur = front(sq, xsrc(0), T)
        for c in range(nch):
            hook = None
            if c + 1 < nch:
                hook = (lambda c=c, sq=sq, T=T: front(sq, xsrc(c + 1), T))
            xt_cur = chunk(sq, xt_cur, ydst(c), T, hook)
        hg_o = (hg_p if kind == "p" else hg_s)[bi]
        ssm_o = (ssm_p if kind == "p" else ssm_s)[bi]
        conv_o = (conv_p if kind == "p" else conv_s)[bi]
        P.dma(hg_o.re("h k v -> k h v"), S, "so_hg", eng="pool")
        for half in range(2):
            pb = pbank()
            for a in range(8):
                P.tr(pb[:, a * 128:(a + 1) * 128], H[:, half * 1024 + a * 128:half * 1024 + (a + 1) * 128], ident)
            tmp = scr[0]
            P.copy(tmp, pb)
            P.dma(ssm_o[half * 1024:(half + 1) * 1024, :].re("(a p) n -> p a n", p=128), tmp.re("p (a n) -> p a n", a=8), "so_ssm", eng="pool")
        c72 = ctmp[1]
        P.copy(c72[:, 0:72].re("p (j a) -> p a j", j=3), xb[:, :, 0:3])
        pb = pbank()
        P.tr(pb[:72, 0:128], c72[:, 0:72], ident)
        c72o = scr[1]
        P.copy(c72o[:72, 0:128], pb[:72, 0:128])
        P.dma(conv_o, c72o[:72, 0:128], "so_cv", eng="pool")

    P.finish(["yout", "so_hg", "so_ssm", "so_cv"])
    return nc, P


_CACHE = {}


def _layout_inputs(inp, core, NP, TP, TS, n_cores):
    f = lambda a: np.ascontiguousarray(a, dtype=np.float32)
    seq_ids = list(range(core * NP, (core + 1) * NP))
    cvec = np.concatenate([inp["c_prompt"][seq_ids], inp["c_sample"][core:core + 1]], axis=0)
    return cvec, seq_ids


def kernel(**inp):
    NP, TP, TS, NC = 2, 4096, 16, 8
    f = lambda a: np.ascontiguousarray(a, dtype=np.float32)
    if "nc" not in _CACHE:
        _CACHE["nc"] = build(NP, TP, TS)
    nc, P = _CACHE["nc"]
    shared = {
        "w_ada": f(inp["w_ada"][0]), "b_ada": f(inp["b_ada"][0]),
        "norm1_w": f(inp["norm1_w"][0].reshape(8, 128)),
        "w_in": f(inp["w_in"][0]),
        "lbs": f(inp["hgrn_lower_bounds"].reshape(16, 128)),
        "hgw": f(inp["hgrn_norm_w"][0].reshape(8, 128)),
        "conv_w": f(inp["conv_w"][0].reshape(96, 128)),
        "conv_b": f(inp["conv_b"][0].reshape(24, 128)),
        "dt_bias": f(inp["dt_bias"][0]), "a_log": f(inp["a_log"][0]), "ssm_d": f(inp["ssm_d"][0]),
        "snw": f(inp["ssm_norm_w"][0].reshape(16, 128)),
        "w_a": f(inp["w_branch_a"][0]), "w_b": f(inp["w_branch_b"][0]), "w_o": f(inp["w_out"][0]),
        "norm2_w": f(inp["norm2_w"][0].reshape(8, 128)),
        "wq": f(inp["peer_wq"][0]),
        "keysT": f(np.concatenate([inp["peer_keys1"][0], inp["peer_keys2"][0]], axis=0).transpose(0, 2, 1)),
        "uT": f(inp["peer_u"][0].T), "pv": f(inp["peer_v"][0]),
        "fnw": f(inp["final_norm_w"]),
    }
    in_maps = []
    for c in range(NC):
        m = dict(shared)
        ids = list(range(c * NP, (c + 1) * NP))
        m["xp"] = f(inp["x_prompt"][ids])
        m["xs"] = f(inp["x_sample"][c:c + 1])
        m["cv"] = f(np.concatenate([inp["c_prompt"][ids], inp["c_sample"][c:c + 1]], axis=0).reshape((NP + 1) * 8, 128))
        m["st_hg"] = f(inp["state_hgrn"][0, c])
        m["st_ssm"] = f(inp["state_ssm"][0, c].reshape(2048, 128))
        m["st_conv"] = f(inp["state_conv"][0, c].reshape(72, 128))
        in_maps.append(m)
    res = run_bass_kernel_spmd(nc, in_maps, core_ids=list(range(NC)))
    R = res.results
    cat = lambda k: np.concatenate([r[k] for r in R], axis=0)
    y_p = cat("y_p")
    y_s = cat("y_s")
    hg_p = cat("hg_p")[None]
    ssm_p = cat("ssm_p").reshape(NC * NP, 32, 64, 128)[None]
    conv_p = cat("conv_p").reshape(NC * NP, 3, 3072)[None]
    hg_s = cat("hg_s")[None]
    ssm_s = cat("ssm_s").reshape(NC, 32, 64, 128)[None]
    conv_s = cat("conv_s").reshape(NC, 3, 3072)[None]
    return (y_p, y_s, hg_p, ssm_p, conv_p, hg_s, ssm_s, conv_s)
```

```python
import numpy as np
import concourse.bass as bass
import concourse.mybir as mybir
from concourse.bass_utils import run_bass_kernel_spmd

F32 = mybir.dt.float32
BF16 = mybir.dt.bfloat16
AF = mybir.ActivationFunctionType
ALU = mybir.AluOpType
AX = mybir.AxisListType

D = 1024
IN_COLS = 11296
EPS = 1e-6
NEG = -1.0e30


class Tok:
    __slots__ = ("w", "r", "name")

    def __init__(self, name=""):
        self.w = None
        self.r = {}
        self.name = name


class V:
    __slots__ = ("ap", "toks")

    def __init__(self, ap, toks):
        self.ap = ap
        self.toks = toks

    def __getitem__(self, idx):
        return V(self.ap[idx], self.toks)

    def re(self, s, **kw):
        return V(self.ap.rearrange(s, **kw), self.toks)

    def bc(self, shape):
        return V(self.ap.to_broadcast(list(shape)), self.toks)

    def un(self, axis):
        return V(self.ap.unsqueeze(axis), self.toks)

    def pbc(self, n):
        return V(self.ap.partition_broadcast(n), self.toks)


class Prog:
    ENG = ("pe", "act", "dve", "pool", "sp")

    def __init__(self, nc):
        self.nc = nc
        self.q = {e: [] for e in self.ENG}
        self.cnt = {}
        self.sems = {}
        self.waited = {e: {} for e in self.ENG}
        self.ctx = []
        self.n_inst = 0
        for e in self.ENG:
            self._mksem("E_" + e)

    def _mksem(self, key):
        if key not in self.sems:
            cm = self.nc.semaphore(key)
            h = cm.__enter__()
            self.ctx.append(cm)
            self.sems[key] = h
            self.cnt[key] = 0
        return self.sems[key]

    def sb(self, name, shape, dtype=F32):
        cm = self.nc.sbuf_tensor(name, list(shape), dtype)
        t = cm.__enter__()
        self.ctx.append(cm)
        return V(t[tuple(slice(None) for _ in shape)], [Tok(name)])

    def ps(self, name, shape, dtype=F32):
        cm = self.nc.psum_tensor(name, list(shape), dtype)
        t = cm.__enter__()
        self.ctx.append(cm)
        return V(t[tuple(slice(None) for _ in shape)], [Tok(name)])

    def dram(self, name, shape, dtype, kind):
        t = self.nc.dram_tensor(name, list(shape), dtype, kind=kind)
        return V(t.ap(), [Tok(name)])

    def _emit(self, eng, fn, reads, writes, dsem=None, relaxed=False):
        deps = {}

        def add(ev):
            if ev is None:
                return
            k, v = ev
            if deps.get(k, 0) < v:
                deps[k] = v

        for x in reads:
            for t in x.toks:
                add(t.w)
        for x in writes:
            for t in x.toks:
                add(t.w)
                for k, v in t.r.items():
                    add((k, v))
        own = "E_" + eng
        waits = []
        for k, v in deps.items():
            if (eng == "pe" or relaxed) and k == own:
                continue
            if self.waited[eng].get(k, 0) >= v:
                continue
            self.waited[eng][k] = v
            waits.append((self.sems[k], v))
        if dsem is not None:
            self._mksem(dsem)
            self.cnt[dsem] += 16
            ev = (dsem, self.cnt[dsem])
            inc = (self.sems[dsem], 16)
        else:
            self.cnt[own] += 1
            ev = (own, self.cnt[own])
            inc = (self.sems[own], 1)
        for x in reads:
            for t in x.toks:
                if t.r.get(ev[0], 0) < ev[1]:
                    t.r[ev[0]] = ev[1]
        for x in writes:
            for t in x.toks:
                t.w = ev
                t.r = {}
        self.q[eng].append((waits, fn, inc))
        self.n_inst += 1
        return ev

    def mm(self, out, lhsT, rhs, start=True, stop=True):
        def fn(e):
            return e.matmul(out.ap, lhsT.ap, rhs.ap, start=start, stop=stop)
        return self._emit("pe", fn, [lhsT, rhs], [out])

    def tr(self, out, in_, ident):
        def fn(e):
            return e.transpose(out.ap, in_.ap, ident.ap)
        return self._emit("pe", fn, [in_, ident], [out])

    def act(self, out, in_, func, bias=None, scale=None, accum=None):
        rd = [in_]
        kw = {}
        if bias is not None:
            if isinstance(bias, V):
                rd.append(bias)
                kw["bias"] = bias.ap
            else:
                kw["bias"] = float(bias)
        if scale is not None:
            if isinstance(scale, V):
                rd.append(scale)
                kw["scale"] = scale.ap
            else:
                kw["scale"] = float(scale)
        wr = [out]
        if accum is not None:
            wr.append(accum)
            kw["accum_out"] = accum.ap

        def fn(e):
            return e.activation(out.ap, in_.ap, func, **kw)
        return self._emit("act", fn, rd, wr)

    def tt(self, out, in0, in1, op, eng="dve"):
        def fn(e):
            return e.tensor_tensor(out.ap, in0.ap, in1.ap, op)
        return self._emit(eng, fn, [in0, in1], [out])

    def ts(self, out, in0, s1, op0, s2=None, op1=None, eng="dve"):
        rd = [in0]
        a1 = s1.ap if isinstance(s1, V) else float(s1)
        if isinstance(s1, V):
            rd.append(s1)
        a2 = None
        if s2 is not None:
            a2 = s2.ap if isinstance(s2, V) else float(s2)
            if isinstance(s2, V):
                rd.append(s2)

        def fn(e):
            if op1 is None:
                return e.tensor_scalar(out.ap, in0.ap, a1, None, op0)
            return e.tensor_scalar(out.ap, in0.ap, a1, a2, op0, op1)
        return self._emit(eng, fn, rd, [out])

    def stt(self, out, in0, scalar, in1, op0, op1):
        rd = [in0, in1]
        a = scalar.ap if isinstance(scalar, V) else float(scalar)
        if isinstance(scalar, V):
            rd.append(scalar)

        def fn(e):
            return e.scalar_tensor_tensor(out.ap, in0.ap, a, in1.ap, op0, op1)
        return self._emit("dve", fn, rd, [out])

    def copy(self, out, in_, eng="dve"):
        if eng == "act":
            return self.act(out, in_, AF.Copy)

        def fn(e):
            return e.tensor_copy(out.ap, in_.ap)
        return self._emit(eng, fn, [in_], [out])

    def memset(self, out, val, eng="dve"):
        def fn(e):
            return e.memset(out.ap, val)
        return self._emit(eng, fn, [], [out])

    def reduce(self, out, in_, op=None, axis=None):
        op = op or ALU.add
        axis = axis or AX.X

        def fn(e):
            return e.tensor_reduce(out.ap, in_.ap, axis, op)
        return self._emit("dve", fn, [in_], [out])

    def recip(self, out, in_):
        def fn(e):
            return e.reciprocal(out.ap, in_.ap)
        return self._emit("dve", fn, [in_], [out])

    def scan(self, out, d0, d1, initial, op0, op1):
        def fn(e):
            return e.tensor_tensor_scan(out.ap, d0.ap, d1.ap, initial, op0, op1)
        return self._emit("dve", fn, [d0, d1], [out])

    def max8(self, out, in_, relaxed=False):
        def fn(e):
            return e.max(out.ap, in_.ap)
        return self._emit("dve", fn, [in_], [out], relaxed=relaxed)

    def mrep(self, out, rep, vals, imm, relaxed=False):
        def fn(e):
            return e.match_replace(out.ap, rep.ap, vals.ap, imm)
        return self._emit("dve", fn, [rep, vals], [out], relaxed=relaxed)

    def asel(self, out, pattern, cmp, fill, base, cm):
        def fn(e):
            return e.affine_select(out=out.ap, in_=out.ap, pattern=pattern, compare_op=cmp, fill=fill,
                                   base=base, channel_multiplier=cm)
        return self._emit("pool", fn, [out], [out])

    def dma(self, out, in_, dsem, eng="sp", **kw):
        def fn(e):
            return e.dma_start(out=out.ap, in_=in_.ap, **kw)
        return self._emit(eng, fn, [in_], [out], dsem=dsem)

    def finish(self, final_keys):
        nc = self.nc
        engmap = {"pe": "tensor", "act": "scalar", "dve": "vector", "pool": "gpsimd", "sp": "sync"}
        with nc.Block() as block:
            for en in self.ENG:
                ops = self.q[en]
                extra = [(k, self.cnt[k]) for k in final_keys] if en == "sp" else []

                def body(e, ops=ops, extra=extra):
                    for waits, fn, inc in ops:
                        for s, v in waits:
                            e.wait_ge(s, v)
                        fn(e).then_inc(inc[0], inc[1])
                    for k, v in extra:
                        e.wait_ge(self.sems[k], v)
                getattr(block, engmap[en])(body)


def build(NP, TP, TS, n_eblk=16):
    nc = bass.Bass("TRN2", target_bir_lowering=False)
    P = Prog(nc)
    NSEQ = NP + 1
    NE = n_eblk * 1024
    EI, EO = "ExternalInput", "ExternalOutput"
    xp = P.dram("xp", [NP, TP, D], F32, EI)
    xs = P.dram("xs", [1, TS, D], F32, EI)
    cv = P.dram("cv", [NSEQ * 8, 128], F32, EI)
    st_hg = P.dram("st_hg", [8, 128, 128], F32, EI)
    st_ssm = P.dram("st_ssm", [2048, 128], F32, EI)
    st_conv = P.dram("st_conv", [72, 128], F32, EI)
    w_ada = P.dram("w_ada", [D, 6 * D], F32, EI)
    b_ada = P.dram("b_ada", [6 * D], F32, EI)
    norm1_w = P.dram("norm1_w", [8, 128], F32, EI)
    w_in = P.dram("w_in", [D, IN_COLS], F32, EI)
    lbs = P.dram("lbs", [16, 128], F32, EI)
    hgw = P.dram("hgw", [8, 128], F32, EI)
    conv_w = P.dram("conv_w", [96, 128], F32, EI)
    conv_b = P.dram("conv_b", [24, 128], F32, EI)
    dt_bias = P.dram("dt_bias", [32], F32, EI)
    a_log = P.dram("a_log", [32], F32, EI)
    ssm_d = P.dram("ssm_d", [32], F32, EI)
    snw = P.dram("snw", [16, 128], F32, EI)
    w_a = P.dram("w_a", [D, D], F32, EI)
    w_b = P.dram("w_b", [2 * D, D], F32, EI)
    w_o = P.dram("w_o", [D, D], F32, EI)
    norm2_w = P.dram("norm2_w", [8, 128], F32, EI)
    wq = P.dram("wq", [D, 2 * D], F32, EI)
    keysT = P.dram("keysT", [16, 128, 128], F32, EI)
    uT = P.dram("uT", [D, NE], F32, EI)
    pv = P.dram("pv", [NE, D], F32, EI)
    fnw = P.dram("fnw", [D], F32, EI)

    y_p = P.dram("y_p", [NP, TP, D], F32, EO)
    y_s = P.dram("y_s", [1, TS, D], F32, EO)
    hg_p = P.dram("hg_p", [NP, 8, 128, 128], F32, EO)
    ssm_p = P.dram("ssm_p", [NP, 2048, 128], F32, EO)
    conv_p = P.dram("conv_p", [NP, 72, 128], F32, EO)
    hg_s = P.dram("hg_s", [1, 8, 128, 128], F32, EO)
    ssm_s = P.dram("ssm_s", [1, 2048, 128], F32, EO)
    conv_s = P.dram("conv_s", [1, 72, 128], F32, EO)

    w_in_b = P.dram("w_in_b", [D, IN_COLS], BF16, "Internal")
    w_a_b = P.dram("w_a_b", [D, D], BF16, "Internal")
    w_b_b = P.dram("w_b_b", [2 * D, D], BF16, "Internal")
    w_o_b = P.dram("w_o_b", [D, D], BF16, "Internal")
    wq_b = P.dram("wq_b", [D, 2 * D], BF16, "Internal")
    uT_b = P.dram("uT_b", [D, NE], BF16, "Internal")
    pv_b = P.dram("pv_b", [NE, D], BF16, "Internal")

    psd = [P.ps("psd%d" % i, [128, 1024], F32) for i in range(4)]
    ps_out = psd[3]
    ring = {"i": 0}

    def pbank():
        b = psd[ring["i"] % 3]
        ring["i"] += 1
        return b

    ident = P.sb("ident", [128, 128], F32)
    P.memset(ident, 0.0)
    P.asel(ident, [[-1, 128]], ALU.not_equal, 1.0, 0, 1)
    identb = P.sb("identb", [128, 128], BF16)
    P.copy(identb, ident)
    U = P.sb("U", [128, 128], F32)
    P.memset(U, 1.0)
    P.asel(U, [[1, 128]], ALU.is_ge, 0.0, 0, -1)
    ones32 = P.sb("ones32", [32, 128], F32)
    P.memset(ones32, 1.0)
    zerob = P.sb("zerob", [128, 512], BF16)
    P.memset(zerob, 0.0)
    ones128 = P.sb("ones128", [128, 128], F32)
    P.memset(ones128, 1.0)

    scr_all = P.sb("scr_all", [128, 8 * 1024], F32)
    scr = [V(scr_all.ap[:, i * 1024:(i + 1) * 1024], [Tok("scr%d" % i)]) for i in range(8)]
    NB = 12
    bsc_all = P.sb("bsc_all", [128, NB * 1024], BF16)
    bsc = [V(bsc_all.ap[:, i * 1024:(i + 1) * 1024], [Tok("bsc%d" % i)]) for i in range(NB)]

    def loadT(dst, src, R, dsem="ldT"):
        tmp = scr[0]
        P.dma(tmp[:R, 0:128], src, dsem)
        pb = pbank()
        P.tr(pb[:, 0:R], tmp[:R, 0:128], ident[:R, :R])
        P.copy(dst, pb[:, 0:R])

    small = P.sb("small", [128, 512], F32)
    o = 0

    def sm(n):
        nonlocal o
        v = small[:, o:o + n]
        o += n
        return v
    cT = sm(NSEQ * 8)
    lbT2 = sm(16)
    lbT = sm(8)
    omlT = sm(8)
    nomlT = sm(8)
    hgwT = sm(8)
    cwT = sm(96)
    cbT = sm(24)
    snwT = sm(16)
    n1T = sm(8)
    n2T = sm(8)
    scT = sm(NSEQ * 8)
    badaT = sm(48)
    dtb_b = sm(32)
    aneg_b = sm(32)
    dsk_b = sm(32)
    oh32 = sm(32)
    modT = P.sb("modT", [128, 48, NSEQ], F32)
    A1T = P.sb("A1T", [128, NSEQ, 8], F32)
    A2T = P.sb("A2T", [128, NSEQ, 8], F32)
    stcT = P.sb("stcT", [128, 72], F32)

    loadT(cT, cv, NSEQ * 8)
    loadT(lbT2, lbs, 16)
    loadT(hgwT, hgw, 8)
    loadT(cwT, conv_w, 96)
    loadT(cbT, conv_b, 24)
    loadT(snwT, snw, 16)
    loadT(n1T, norm1_w, 8)
    loadT(n2T, norm2_w, 8)
    loadT(badaT, b_ada.re("(a p) -> a p", p=128), 48)
    loadT(stcT, st_conv, 72)
    P.copy(oh32, ident[:, 0:32])
    P.tt(lbT, lbT2[:, 0:8], lbT2[:, 8:16], ALU.subtract)
    P.act(lbT, lbT, AF.Sigmoid)
    P.ts(omlT, lbT, -1.0, ALU.mult, 1.0, ALU.add)
    P.ts(nomlT, omlT, -1.0, ALU.mult)
    P.dma(dtb_b, dt_bias.pbc(128), "ld_a")
    P.dma(aneg_b, a_log.pbc(128), "ld_b")
    P.dma(dsk_b, ssm_d.pbc(128), "ld_c")
    P.act(aneg_b, aneg_b, AF.Exp)
    P.ts(aneg_b, aneg_b, -1.0, ALU.mult)
    fnw_b = P.sb("fnw_b", [128, D], F32)
    P.dma(fnw_b, fnw.pbc(128), "ld_d")
    keysb = P.sb("keysb", [128, 16, 128], BF16)
    for a in range(2):
        P.dma(scr[1].re("p (a n) -> p a n", a=8), keysT[a * 8:(a + 1) * 8].re("a d n -> d a n"), "ld_e")
        P.copy(keysb[:, a * 8:(a + 1) * 8, :], scr[1].re("p (a n) -> p a n", a=8))
    def cvt(dst, src, rows, step, key):
        for r0 in range(0, rows, step):
            P.dma(dst[r0:r0 + step], src[r0:r0 + step], key, eng="pool")
        dst.toks[0].w = (key, P.cnt[key])
    cvt(w_in_b, w_in, D, 128, "cv_in")
    cvt(w_a_b, w_a, D, 256, "cv_a")
    cvt(w_b_b, w_b, 2 * D, 256, "cv_b")
    cvt(w_o_b, w_o, D, 256, "cv_o")
    cvt(wq_b, wq, D, 256, "cv_q")
    cvt(uT_b, uT, D, 64, "cv_u")
    cvt(pv_b, pv, NE, 256, "cv_v")

    P.act(scT, cT, AF.Silu)

    NSLOT = 4
    wslots = [P.sb("wslot%d" % i, [128, 8, 1024], BF16) for i in range(NSLOT)]
    wr = {"i": 0}

    def wload(src_rows_cols, ncols, dsem_extra=""):
        i = wr["i"] % NSLOT
        wr["i"] += 1
        s = wslots[i]
        P.dma(s[:, :, 0:ncols], src_rows_cols.re("(k p) c -> p k c", p=128), "ws%d" % i)
        return s

    ada_f = [V(wslots[i].ap.bitcast(F32), wslots[i].toks) for i in range(2)]

    def ada_load(i, c0):
        s = ada_f[i % 2]
        P.dma(s, w_ada[:, c0:c0 + 512].re("(k p) c -> p k c", p=128), "ada%d" % (i % 2))
        return s
    fm_tiles = list(range(0, 16)) + list(range(24, 40))
    li = 0
    for b0 in (0, 512, 1024, 1536, 3072, 3584, 4096, 4608):
        s = ada_load(li, b0)
        li += 1
        pb = pbank()
        for j in range(4):
            tile = b0 // 128 + j
            for k in range(8):
                P.mm(pb[:, j * NSEQ:(j + 1) * NSEQ], s[:, k, j * 128:(j + 1) * 128],
                     scT.re("p (s k) -> p k s", k=8)[:, k, :], start=(k == 0), stop=(k == 7))
            P.tt(modT[:, tile, :], pb[:, j * NSEQ:(j + 1) * NSEQ], badaT[:, tile:tile + 1].bc([128, NSEQ]), ALU.add)
    for sq in range(NSEQ):
        P.stt(A1T[:, sq, :], modT[:, 8:16, sq], 1.0, n1T, ALU.add, ALU.mult)
        P.stt(A2T[:, sq, :], modT[:, 32:40, sq], 1.0, n2T, ALU.add, ALU.mult)

    g1b = P.sb("g1b", [128, D], F32)
    g2b = P.sb("g2b", [128, D], F32)
    S = P.sb("S", [128, 8, 128], F32)
    H = P.sb("H", [128, 2048], F32)
    Hb = P.sb("Hb", [128, 2048], BF16)
    xb = P.sb("xb", [128, 24, 131], F32)
    xt_ring = [P.sb("xt%d" % i, [128, D], F32) for i in range(2)]
    scb = scr[5].re("p (k t) -> p k t", k=8)
    stat = P.sb("stat", [128, 64], F32)
    hT = P.sb("hT", [128, 8, 128], BF16)
    h2T = P.sb("h2T", [128, 8, 128], BF16)
    xcs = P.sb("xcs", [128, 24, 128], BF16)
    ctmp = [P.sb("ctmp%d" % i, [128, 128], F32) for i in range(2)]
    ctmp2 = [P.sb("ctmpb%d" % i, [128, 128], F32) for i in range(2)]
    junk = P.sb("junk", [128, 1024], BF16)
    att_m = [P.sb("attm%d" % i, [128, 128], BF16) for i in range(2)]
    Smb = [P.sb("Smb%d" % i, [128, 128], BF16) for i in range(2)]
    tmpS = [P.sb("tmpS%d" % i, [128, 128], F32) for i in range(2)]
    gst = P.sb("gst", [128, 40], F32)
    dts = P.sb("dts", [128, 6, 32], F32)
    acT = P.sb("acT", [32, 128], F32)
    Xgf = V(scr[0].ap[:32, :], scr[0].toks)
    negm = {}
    for T_ in sorted(set([128, TS])):
        nm_ = P.sb("negm%d" % T_, [128, 4 * T_], F32)
        for r_ in range(4):
            P.ts(nm_[:, r_ * T_:(r_ + 1) * T_], U[:, 0:T_], -1.0, ALU.add, 1.0e30, ALU.mult)
        negm[T_] = nm_
    eal = P.sb("eal", [128, 32], F32)
    dgl = P.sb("dgl", [32, 32], F32)
    s12 = V(scr_all.ap[:, 6 * 1024:8 * 1024].rearrange("p (a n) -> p a n", a=16), scr[6].toks + scr[7].toks)
    t16 = P.sb("t16", [128, 16, 16], F32)
    wk = P.sb("wk", [128, 128], F32)
    cand = V(scr_all.ap[:, 4 * 1024:6 * 1024].rearrange("p (h c) -> p h c", h=8), scr[4].toks + scr[5].toks)
    cwk = P.sb("cwk", [128, 256], F32)
    c16 = P.sb("c16", [128, 8, 16], F32)
    pst = P.sb("pst", [128, 8, 8], F32)
    qT = V(bsc_all.ap[:, 9 * 1024:11 * 1024].rearrange("p (a t) -> p a t", a=16), bsc[9].toks + bsc[10].toks)

    def rms_rstd(dst, ss, n):
        P.act(dst, ss, AF.Sqrt, bias=EPS, scale=1.0 / n)
        P.recip(dst, dst)

    xin_cnt = {"i": 0}

    def norm_to_T(xsrc, T, AT, shT, dstT):
        ss = stat[:T, 0:1]
        P.act(junk[:T, :], xsrc[:T, :], AF.Square, accum=ss)
        rs = stat[:T, 1:2]
        rms_rstd(rs, ss, D)
        xn = scr[7]
        P.ts(xn[:T, :], xsrc[:T, :], rs, ALU.mult)
        pb = pbank()
        for k in range(8):
            P.tr(pb[:, k * T:(k + 1) * T], xn[:T, k * 128:(k + 1) * 128], ident[:T, :T])
        for k in range(8):
            P.act(dstT[:, k, :T], pb[:, k * T:(k + 1) * T], AF.Identity, bias=shT[:, k:k + 1], scale=AT[:, k:k + 1])

    def fm_block_T(c0, T, ntile=8):
        s = wload(w_in_b[:, c0:c0 + ntile * 128], ntile * 128)
        pb = pbank()
        for j in range(ntile):
            for k in range(8):
                P.mm(pb[:, j * T:(j + 1) * T], s[:, k, j * 128:(j + 1) * 128], hT[:, k, :T],
                     start=(k == 0), stop=(k == 7))
        return pb

    def front(sq, xsrc_dram, T):
        xt = xt_ring[xin_cnt["i"] % 2]
        xin_cnt["i"] += 1
        P.dma(xt[:T, :], xsrc_dram, "xin%d" % (xin_cnt["i"] % 2))
        norm_to_T(xt, T, A1T[:, sq, :], modT[:, 0:8, sq], hT)
        pb = fm_block_T(0, T)
        P.act(scr[0][:, 0:8 * T], pb[:, 0:8 * T], AF.Silu)
        pb = fm_block_T(1024, T)
        P.act(scr[1][:, 0:8 * T], pb[:, 0:8 * T], AF.Sigmoid)
        return xt

    def chunk(sq, xt, ydst_dram, T, hook):
        hm = T // 2 - 1

        def fm_block(c0, ntile=8):
            return fm_block_T(c0, T, ntile)

        def tm_block(wsrc, c0, ncols, lhs):
            s = wload(wsrc[:, c0:c0 + ncols], ncols)
            pb = pbank()
            for h0 in range(0, ncols, 512):
                w = min(512, ncols - h0)
                for k in range(8):
                    P.mm(pb[:T, h0:h0 + w], lhs[:, k, :T], s[:, k, h0:h0 + w], start=(k == 0), stop=(k == 7))
            return pb

        qs, sg, lf, g, dd, eq, ek = scr[0], scr[1], scr[2], scr[3], scr[4], scr[5], scr[6]
        v3 = lambda x: V(x.ap[:, 0:8 * T].rearrange("p (h t) -> p h t", h=8), x.toks)
        f2 = lambda x: x[:, 0:8 * T]
        for h in range(8):
            P.act(v3(lf)[:, h, :], v3(sg)[:, h, :], AF.Ln, bias=lbT[:, h:h + 1], scale=omlT[:, h:h + 1])
        for h in range(8):
            P.ts(v3(sg)[:, h, :], v3(sg)[:, h, :], nomlT[:, h:h + 1], ALU.mult, omlT[:, h:h + 1], ALU.add)
        for h in range(8):
            P.scan(v3(g)[:, h, :], ones128[:, :T], v3(lf)[:, h, :], 0.0, ALU.mult, ALU.add)
        gm = gst[:, 0:8]
        gl = gst[:, 8:16]
        P.copy(gm, v3(g)[:, :, hm])
        P.copy(gl, v3(g)[:, :, T - 1])
        P.tt(v3(dd), v3(g), gm.un(2).bc([128, 8, T]), ALU.subtract)
        P.act(f2(eq), f2(dd), AF.Exp)
        P.act(f2(ek), f2(dd), AF.Exp, scale=-1.0)
        egm, eL, eLM = gst[:, 16:24], gst[:, 24:32], gst[:, 32:40]
        P.act(egm, gm, AF.Exp)
        P.act(eL, gl, AF.Exp)
        P.copy(eLM, v3(eq)[:, :, T - 1])
        QT, KT, Kt = bsc[0], bsc[1], bsc[2]
        P.tt(f2(QT), f2(qs), f2(eq), ALU.mult)
        P.tt(f2(KT), f2(sg), f2(ek), ALU.mult)
        vv, sgl, sz0, sz1, sga, sgb = bsc[3], bsc[4], bsc[5], bsc[6], bsc[7], bsc[8]
        pb = tm_block(w_in_b, 2048, 1024, hT)
        P.copy(vv[:T, :], pb[:T, :], eng="act")
        pb = tm_block(w_in_b, 3072, 1024, hT)
        P.act(sgl[:T, :], pb[:T, :], AF.Silu)
        pb = tm_block(w_in_b, 9248, 1024, hT)
        P.act(sga[:T, :], pb[:T, :], AF.Sigmoid)

        pbk = pbank()
        pbkb = V(pbk.ap.bitcast(BF16), pbk.toks)
        for h in range(8):
            P.tr(pbkb[:T, h * 128:(h + 1) * 128], v3(KT)[:, h, :], identb)
        P.copy(Kt[:T, :], pbkb[:T, 0:1024], eng="act")
        po = pbank()
        pu = pbank()
        patt = pbank()
        for h in range(8):
            P.mm(patt[:T, h * 128:h * 128 + T], v3(KT)[:, h, :], v3(QT)[:, h, :])
        am_all, smb_all = bsc[5], bsc[6]
        am3 = V(am_all.ap[:T, 0:8 * T].rearrange("p (h i) -> p h i", h=8), am_all.toks)
        P.tt(am3, patt[:T, :].re("p (h c) -> p h c", h=8)[:, :, 0:T], U[:T, :T].un(1).bc([T, 8, T]), ALU.mult)
        P.tt(smb_all.re("p (h v) -> p h v", h=8), S, egm.un(2).bc([128, 8, 128]), ALU.mult)
        for h in range(8):
            oc = slice(h * 128, (h + 1) * 128)
            P.mm(po[:T, oc], am3[:, h, :], vv[:T, oc], start=True, stop=False)
            P.mm(po[:T, oc], v3(QT)[:, h, :], smb_all[:, oc], start=False, stop=True)
        for h in range(8):
            oc = slice(h * 128, (h + 1) * 128)
            P.mm(pu[:, oc], Kt[:T, oc], vv[:T, oc])
        tsa = scr[2]
        P.tt(tsa.re("p (h v) -> p h v", h=8), pu.re("p (h v) -> p h v", h=8), eLM.un(2).bc([128, 8, 128]), ALU.mult)
        P.tt(S, S, eL.un(2).bc([128, 8, 128]), ALU.mult)
        P.tt(S, S, tsa.re("p (h v) -> p h v", h=8), ALU.add)
        for h in range(8):
            P.act(junk[:T, 0:128], po[:T, h * 128:(h + 1) * 128], AF.Square, accum=stat[:T, 8 + h:9 + h])
        rs8 = stat[:T, 16:24]
        rms_rstd(rs8, stat[:T, 8:16], 128)
        on = scr[0]
        P.tt(on[:T, :].re("p (h v) -> p h v", h=8), po[:T, :].re("p (h v) -> p h v", h=8),
             rs8.un(2).bc([T, 8, 128]), ALU.mult)
        P.tt(on[:T, :], on[:T, :], sgl[:T, :], ALU.mult)
        pb = pbank()
        for k in range(8):
            P.tr(pb[:, k * T:(k + 1) * T], on[:T, k * 128:(k + 1) * 128], ident[:T, :T])
        obT = bsc[0]
        for k in range(8):
            P.act(obT[:, k * T:(k + 1) * T], pb[:, k * T:(k + 1) * T], AF.Identity, scale=hgwT[:, k:k + 1])
        obT3 = V(obT.ap[:, 0:8 * T].rearrange("p (k t) -> p k t", k=8), obT.toks)
        ppa = tm_block(w_a_b, 0, 1024, obT3)
        t1 = scr[1]
        P.tt(t1[:T, :], ppa[:T, :], sga[:T, :], ALU.mult)

        for b in range(3):
            pb = fm_block(6144 + b * 1024)
            P.copy(xb[:, b * 8:(b + 1) * 8, 3:3 + T], pb[:, 0:8 * T].re("p (a t) -> p a t", a=8), eng="act")
            bs = slice(b * 8, (b + 1) * 8)
            cacc = V(scr[3].ap[:, 0:8 * T].rearrange("p (a t) -> p a t", a=8), scr[3].toks)
            ctm = V(scr[4].ap[:, 0:8 * T].rearrange("p (a t) -> p a t", a=8), scr[4].toks)
            P.tt(cacc, xb[:, bs, 0:T], cwT[:, b * 8:(b + 1) * 8].un(2).bc([128, 8, T]), ALU.mult)
            for j in range(1, 4):
                P.tt(ctm, xb[:, bs, j:j + T], cwT[:, j * 24 + b * 8:j * 24 + (b + 1) * 8].un(2).bc([128, 8, T]), ALU.mult)
                P.tt(cacc, cacc, ctm, ALU.add)
            P.tt(cacc, cacc, cbT[:, b * 8:(b + 1) * 8].un(2).bc([128, 8, T]), ALU.add)
            P.act(xcs[:, bs, :T], cacc, AF.Silu)
        pb = tm_block(w_in_b, 4096, 1024, hT)
        P.act(sz0[:T, :], pb[:T, :], AF.Silu)
        pb = tm_block(w_in_b, 5120, 1024, hT)
        P.act(sz1[:T, :], pb[:T, :], AF.Silu)
        s = wload(w_in_b[:, 9216:9248], 32)
        pb = pbank()
        for k in range(8):
            P.mm(pb[:T, 0:32], hT[:, k, :T], s[:, k, 0:32], start=(k == 0), stop=(k == 7))
        dtv, dav, acv, eac = dts[:, 0, :], dts[:, 1, :], dts[:, 2, :], dts[:, 3, :]
        P.tt(dtv[:T], pb[:T, 0:32], dtb_b[:T], ALU.add)
        P.act(dtv[:T], dtv[:T], AF.Exp)
        P.act(dtv[:T], dtv[:T], AF.Ln, bias=1.0)
        P.tt(dav[:T], dtv[:T], aneg_b[:T], ALU.mult)
        pb = tm_block(w_in_b, 10272, 1024, hT)
        P.act(sgb[:T, :], pb[:T, :], AF.Sigmoid)


        pb = pbank()
        P.mm(pb[:T, 0:32], U[:T, :T], dav[:T])
        P.mm(pb[:32, 512:512 + T], dav[:T], U[:T, :T])
        P.copy(acv[:T], pb[:T, 0:32])
        P.copy(acT[:, :T], pb[:32, 512:512 + T])
        P.act(eac[:T], acv[:T], AF.Exp)
        P.ts(dgl, oh32[:32, :], acT[:, T - 1:T], ALU.mult)
        P.mm(pb[:, 256:288], ones32[:, :], dgl)
        P.act(eal, pb[:, 256:288], AF.Exp)
        Btm = bsc[2]
        pbx = pbank()
        pbxb = V(pbx.ap.bitcast(BF16), pbx.toks)
        for tl in range(16):
            P.tr(pbxb[:T, tl * 128:(tl + 1) * 128], xcs[:, tl, :T], identb)
        xtmA = scr[2]
        xtm_v = V(xtmA.ap.bitcast(BF16), xtmA.toks)
        P.copy(xtm_v[:T, :], pbxb[:T, :], eng="act")
        pbB = pbank()
        pbBb = V(pbB.ap.bitcast(BF16), pbB.toks)
        for gq in range(4):
            P.tr(pbBb[:T, gq * 128:(gq + 1) * 128], xcs[:, 16 + gq, :T], identb)
        P.copy(Btm[:T, 0:512], pbBb[:T, 0:512], eng="act")
        xdt_v = V(scr[3].ap.bitcast(BF16), scr[3].toks)
        xD_v = V(scr[4].ap.bitcast(BF16), scr[4].toks)
        r3 = lambda x: x[:T, :].re("p (r q) -> p r q", r=32)
        P.tt(r3(xdt_v), r3(xtm_v), dtv[:T].un(2).bc([T, 32, 64]), ALU.mult)
        P.tt(r3(xD_v), r3(xtm_v), dsk_b[:T].un(2).bc([T, 32, 64]), ALU.mult)
        yb = [scr[5], scr[6]]
        wT_r = [bsc[11], bsc[9]]
        xw_r = [bsc[1], bsc[10]]
        Dm = scr[7]

        def stageA(gq):
            BT = xcs[:, 16 + gq, :T]
            CT = xcs[:, 20 + gq, :T]
            pc = pbank()
            P.mm(pc[:T, 0:T], BT, CT)
            cbmm = tmpS[gq % 2]
            P.tt(cbmm[:T, :T], pc[:T, 0:T], U[:T, :T], ALU.mult)
            P.tt(V(Xgf.ap[:, 0:8 * T].rearrange("p (r i) -> p r i", r=8), Xgf.toks), acT[:, :T].un(1).bc([32, 8, T]),
                 oh32[:32, gq * 8:(gq + 1) * 8].un(2).bc([32, 8, T]), ALU.mult)
            pd_ = pbank()
            for hf in range(2):
                osl = slice(hf * 4 * T, (hf + 1) * 4 * T)
                P.mm(pd_[:T, osl], ones32[:, :T], Xgf[:, osl], start=True, stop=False)
                P.mm(pd_[:T, osl], ident[:T, :T], negm[T][:T, :], start=False, stop=True)
            P.tt(V(Dm.ap[:T, 0:8 * T].rearrange("p (r i) -> p r i", r=8), Dm.toks),
                 pd_[:T, 0:8 * T].re("p (r i) -> p r i", r=8),
                 acv[:T, gq * 8:(gq + 1) * 8].un(2).bc([T, 8, T]), ALU.subtract)
            P.act(Dm[:T, 0:8 * T], Dm[:T, 0:8 * T], AF.Exp)
            E3 = V(Dm.ap[:T, 0:8 * T].rearrange("p (r i) -> p r i", r=8), Dm.toks)
            wT = wT_r[gq % 2]
            wT3 = V(wT.ap[:T, 0:8 * T].rearrange("p (r i) -> p r i", r=8), wT.toks)
            P.tt(wT3, E3, cbmm[:T, :T].un(1).bc([T, 8, T]), ALU.mult)
            xw = xw_r[gq % 2]
            gsl = slice(gq * 512, (gq + 1) * 512)
            P.tt(xw[:T, 0:512].re("p (r q) -> p r q", r=8), xdt_v[:T, gsl].re("p (r q) -> p r q", r=8),
                 E3[:, :, T - 1:T].bc([T, 8, 64]), ALU.mult)

        def stageB(gq):
            CT = xcs[:, 20 + gq, :T]
            wT = wT_r[gq % 2]
            wT3 = V(wT.ap[:T, 0:8 * T].rearrange("p (r i) -> p r i", r=8), wT.toks)
            xw = xw_r[gq % 2]
            gsl = slice(gq * 512, (gq + 1) * 512)
            pin = pbank()
            P.mm(pin[:T, 0:512], identb[:T, :T], xD_v[:T, gsl], start=True, stop=False)
            for r in range(8):
                P.mm(pin[:T, r * 64:(r + 1) * 64], wT3[:, r, :], xdt_v[:T, gq * 512 + r * 64:gq * 512 + (r + 1) * 64],
                     start=False, stop=(r == 7))
            P.mm(pin[:T, 512:1024], CT, Hb[:, gsl])
            yg = yb[gq // 2][:T, (gq % 2) * 512:(gq % 2 + 1) * 512]
            P.tt(yg.re("p (r q) -> p r q", r=8), pin[:T, 512:1024].re("p (r q) -> p r q", r=8),
                 eac[:T, gq * 8:(gq + 1) * 8].un(2).bc([T, 8, 64]), ALU.mult)
            P.tt(yg, yg, pin[:T, 0:512], ALU.add)
            szg = (sz0 if gq < 2 else sz1)[:T, (gq % 2) * 512:(gq % 2 + 1) * 512]
            P.tt(yg, yg, szg, ALU.mult)
            P.act(junk[:T, 0:512], yg, AF.Square, accum=stat[:T, 24 + gq:25 + gq])
            pup = pbank()
            P.mm(pup[:, 0:512], Btm[:T, gq * 128:(gq + 1) * 128], xw[:T, 0:512])
            P.tt(H[:, gsl].re("p (r q) -> p r q", r=8), H[:, gsl].re("p (r q) -> p r q", r=8),
                 eal[:, gq * 8:(gq + 1) * 8].un(2).bc([128, 8, 64]), ALU.mult)
            P.tt(H[:, gsl], H[:, gsl], pup[:, 0:512], ALU.add)
            P.copy(Hb[:, gsl], H[:, gsl], eng="act")

        stageA(0)
        stageA(1)
        stageB(0)
        stageA(2)
        stageB(1)
        stageA(3)
        stageB(2)
        stageB(3)
        rs4 = stat[:T, 28:32]
        rms_rstd(rs4, stat[:T, 24:28], 512)
        for half in range(2):
            yh = yb[half]
            P.tt(yh[:T, :].re("p (a q) -> p a q", a=2), yh[:T, :].re("p (a q) -> p a q", a=2),
                 rs4[:, half * 2:half * 2 + 2].un(2).bc([T, 2, 512]), ALU.mult)
        ynT = [bsc[0], bsc[2]]
        for half in range(2):
            pb = pbank()
            for k in range(8):
                P.tr(pb[:, k * T:(k + 1) * T], yb[half][:T, k * 128:(k + 1) * 128], ident[:T, :T])
            for k in range(8):
                P.act(ynT[half][:, k * T:(k + 1) * T], pb[:, k * T:(k + 1) * T], AF.Identity,
                      scale=snwT[:, half * 8 + k:half * 8 + k + 1])
        ppb = pbank()
        for half in range(2):
            s = wload(w_b_b[half * 1024:(half + 1) * 1024, :], 1024)
            yv = V(ynT[half].ap[:, 0:8 * T].rearrange("p (k t) -> p k t", k=8), ynT[half].toks)
            for h0 in (0, 512):
                for k in range(8):
                    P.mm(ppb[:T, h0:h0 + 512], yv[:, k, :], s[:, k, h0:h0 + 512], start=(half == 0 and k == 0),
                         stop=(half == 1 and k == 7))
        t2 = scr[2]
        P.tt(t2[:T, :], ppb[:T, :], sgb[:T, :], ALU.mult)
        P.tt(t1[:T, :], t1[:T, :], t2[:T, :], ALU.add)
        pb = pbank()
        for k in range(8):
            P.tr(pb[:, k * T:(k + 1) * T], t1[:T, k * 128:(k + 1) * 128], ident[:T, :T])
        mT = bsc[1]
        P.copy(mT[:, 0:8 * T], pb[:, 0:8 * T], eng="act")
        mT3 = V(mT.ap[:, 0:8 * T].rearrange("p (k t) -> p k t", k=8), mT.toks)
        pmix = tm_block(w_o_b, 0, 1024, mT3)
        P.tt(t2[:T, :], pmix[:T, :], g1b[:T, :], ALU.mult)
        P.tt(xt[:T, :], xt[:T, :], t2[:T, :], ALU.add)
        P.copy(ctmp[0][:, 0:72].re("p (a j) -> p a j", a=24), xb[:, :, T:T + 3])
        P.copy(xb[:, :, 0:3], ctmp[0][:, 0:72].re("p (a j) -> p a j", a=24))

        norm_to_T(xt, T, A2T[:, sq, :], modT[:, 24:32, sq], h2T)
        for b in range(2):
            s = wload(wq_b[:, b * 1024:(b + 1) * 1024], 1024)
            pb = pbank()
            for j in range(8):
                for k in range(8):
                    P.mm(pb[:, j * T:(j + 1) * T], s[:, k, j * 128:(j + 1) * 128], h2T[:, k, :T],
                         start=(k == 0), stop=(k == 7))
            P.copy(qT[:, b * 8:(b + 1) * 8, :T], pb[:, 0:8 * T].re("p (a t) -> p a t", a=8), eng="act")
        for b in range(2):
            pb = pbank()
            for j in range(8):
                ct = b * 8 + j
                hh, half = ct // 2, ct % 2
                P.mm(pb[:T, j * 128:(j + 1) * 128], qT[:, ct, :T], keysb[:, half * 8 + hh, :])
            P.copy(s12[:T, b * 8:(b + 1) * 8, :], pb[:T, :].re("p (a n) -> p a n", a=8), eng="act")
        wk16 = V(scr_all.ap[:, 0:2048].rearrange("p (a n) -> p a n", a=16), scr[0].toks + scr[1].toks)
        for ct in range(16):
            P.max8(t16[:T, ct, 0:8], s12[:T, ct, :], relaxed=(ct > 0))
        for ct in range(16):
            P.mrep(wk16[:T, ct, :], t16[:T, ct, 0:8], s12[:T, ct, :], NEG, relaxed=(ct > 0))
        for ct in range(16):
            P.max8(t16[:T, ct, 8:16], wk16[:T, ct, :], relaxed=(ct > 0))
        t4 = t16[:T].re("p (h a) k -> p h a k", a=2)
        for h in range(8):
            P.tt(cand[:T, h, :].re("p (a b) -> p a b", a=16), t4[:, h, 0, :].un(2).bc([T, 16, 16]),
                 t4[:, h, 1, :].un(1).bc([T, 16, 16]), ALU.add)
        cwk8 = V(scr_all.ap[:, 2048:4096].rearrange("p (h c) -> p h c", h=8), scr[2].toks + scr[3].toks)
        for h in range(8):
            P.max8(c16[:T, h, 0:8], cand[:T, h, :], relaxed=(h > 0))
        for h in range(8):
            P.mrep(cwk8[:T, h, :], c16[:T, h, 0:8], cand[:T, h, :], NEG, relaxed=(h > 0))
        for h in range(8):
            P.max8(c16[:T, h, 8:16], cwk8[:T, h, :], relaxed=(h > 0))
        tau = pst[:T, :, 0]
        mx = pst[:T, :, 1]
        Zs = pst[:T, :, 2]
        nb = pst[:T, :, 3]
        P.copy(tau, c16[:T, :, 15])
        P.copy(mx, c16[:T, :, 0])
        e16 = cwk[:T, 0:128].re("p (h k) -> p h k", h=8)
        P.tt(e16, c16[:T], mx.un(2).bc([T, 8, 16]), ALU.subtract)
        P.act(e16, e16, AF.Exp)
        P.reduce(Zs, e16)
        P.act(Zs, Zs, AF.Ln)
        P.tt(nb, mx, Zs, ALU.add)
        P.ts(nb, nb, -1.0, ALU.mult)
        xd_r = [scr[0], scr[1], scr[3], scr[4]]
        POOL_H = (5,)
        ex_r = [bsc[0], bsc[1], bsc[11]]
        Wh_r = [bsc[2], bsc[3], bsc[4]]
        gA_r = [bsc[5], bsc[6]]
        GT_r = [bsc[7], bsc[8]]
        pw_r = [psd[0], psd[1]]
        pa = psd[2]
        NS = n_eblk * 8
        wsl = {}

        def L(eb):
            su = wload(uT_b[:, eb * 1024:(eb + 1) * 1024], 1024)
            i = wr["i"] % NSLOT
            wr["i"] += 1
            sv = wslots[i]
            P.dma(sv, pv_b[eb * 1024:(eb + 1) * 1024, :].re("(a p) c -> p a c", p=128), "ws%d" % i)
            wsl[eb] = (su, sv)

        def P1(s_):
            eb, h = divmod(s_, 8)
            xd = xd_r[s_ % 4]
            P.tt(xd[:T, :].re("p (i j) -> p i j", i=8), s12[:T, 2 * h, eb * 8:(eb + 1) * 8].un(2).bc([T, 8, 128]),
                 s12[:T, 2 * h + 1, :].un(1).bc([T, 8, 128]), ALU.add, eng=("pool" if h in POOL_H else "dve"))
            P.act(ex_r[s_ % 3][:T, :], xd[:T, :], AF.Exp, bias=pst[:T, h, 3:4])

        def P3(s_):
            eb, h = divmod(s_, 8)
            xd, ex, Wh = xd_r[s_ % 4], ex_r[s_ % 3], Wh_r[s_ % 3]
            P.stt(Wh[:T, :], xd[:T, :], pst[:T, h, 0:1], ex[:T, :], ALU.is_ge, ALU.mult)
            pw = pw_r[eb % 2]
            if h == 0:
                for h0 in (0, 512):
                    P.mm(pw[:, h0:h0 + 512], zerob[:, 0:128], zerob[:, 0:512], start=True, stop=False)
            for a in range(8):
                P.mm(pw[:, a * T:(a + 1) * T], Wh[:T, a * 128:(a + 1) * 128], identb[:T, :T], start=False,
                     stop=(h == 7 and (a == 7 or (a + 1) * T % 512 == 0)))

        def Atile(eb, a):
            su, sv = wsl[eb]
            for k in range(8):
                P.mm(pa[:, a * T:(a + 1) * T], su[:, k, a * 128:(a + 1) * 128], h2T[:, k, :T], start=(k == 0), stop=(k == 7))
            if a == 7:
                P.act(gA_r[eb % 2][:, 0:8 * T], pa[:, 0:8 * T], AF.Gelu)

        def Gstage(eb):
            P.tt(GT_r[eb % 2][:, 0:8 * T], pw_r[eb % 2][:, 0:8 * T], gA_r[eb % 2][:, 0:8 * T], ALU.mult)

        def Ostage(eb, alist):
            su, sv = wsl[eb]
            GT = GT_r[eb % 2]
            for a in alist:
                for h0 in (0, 512):
                    P.mm(ps_out[:T, h0:h0 + 512], GT[:, a * T:(a + 1) * T], sv[:, a, h0:h0 + 512],
                         start=(eb == 0 and a == 0), stop=(eb == n_eblk - 1 and a == 7))

        L(0)
        if n_eblk > 1:
            L(1)
        P1(0)
        P1(1)
        for s_ in range(NS):
            eb, h = divmod(s_, 8)
            if s_ + 2 < NS:
                P1(s_ + 2)
            P3(s_)
            Atile(eb, h)
            if h == 3 and eb >= 1:
                Gstage(eb - 1)
            if h >= 4 and eb >= 1:
                Ostage(eb - 1, [2 * (h - 4), 2 * (h - 4) + 1])
                if h == 7 and eb + 1 < n_eblk:
                    L(eb + 1)
        Gstage(n_eblk - 1)
        nxt = hook() if hook is not None else None
        Ostage(n_eblk - 1, list(range(8)))
        t2 = scr[2]
        P.tt(t2[:T, :], ps_out[:T, :], g2b[:T, :], ALU.mult)
        P.tt(xt[:T, :], xt[:T, :], t2[:T, :], ALU.add)
        P.act(junk[:T, :], xt[:T, :], AF.Square, accum=stat[:T, 32:33])
        rms_rstd(stat[:T, 33:34], stat[:T, 32:33], D)
        yo = scr[5]
        P.stt(yo[:T, :], xt[:T, :], stat[:T, 33:34], fnw_b[:T, :], ALU.mult, ALU.mult)
        P.dma(ydst_dram, yo[:T, :], "yout", eng="pool")
        return nxt

    seqs = [(i, "p") for i in range(NP)] + [(0, "s")]
    for sq, (bi, kind) in enumerate(seqs):
        T = 128 if kind == "p" else TS
        nch = TP // 128 if kind == "p" else 1
        P.copy(scb, scT[:, sq * 8:(sq + 1) * 8].un(2).bc([128, 8, 128]))
        for gi, (dst, c0) in enumerate(((g1b, 2048), (g2b, 5120))):
            for hb in range(2):
                s = ada_load(li, c0 + hb * 512)
                li += 1
                bb = scr[6]
                P.dma(bb[:, 0:512], b_ada[c0 + hb * 512:c0 + (hb + 1) * 512].pbc(128), "ld_bb")
                pb = pbank()
                for k in range(8):
                    P.mm(pb[:, 0:512], scb[:, k, :], s[:, k, :], start=(k == 0), stop=(k == 7))
                P.tt(dst[:, hb * 512:(hb + 1) * 512], pb[:, 0:512], bb[:, 0:512], ALU.add)
        if kind == "p":
            P.memset(S, 0.0)
            P.memset(H, 0.0)
            P.memset(Hb, 0.0)
            P.memset(xb[:, :, 0:3], 0.0)
        else:
            P.dma(S, st_hg.re("h k v -> k h v"), "ld_S")
            for half in range(2):
                tmp = scr[0]
                P.dma(tmp.re("p (a n) -> p a n", a=8), st_ssm[half * 1024:(half + 1) * 1024, :].re("(a p) n -> p a n", p=128), "ld_ssm")
                pb = pbank()
                for a in range(8):
                    P.tr(pb[:, a * 128:(a + 1) * 128], tmp[:, a * 128:(a + 1) * 128], ident)
                P.copy(H[:, half * 1024:(half + 1) * 1024], pb)
            P.copy(Hb, H)
            P.copy(xb[:, :, 0:3], stcT.re("p (j a) -> p a j", j=3))
        if kind == "p":
            xsrc = lambda c, bi=bi: xp[bi, c * 128:(c + 1) * 128, :]
            ydst = lambda c, bi=bi: y_p[bi, c * 128:(c + 1) * 128, :]
        else:
            xsrc = lambda c: xs[0, :, :]
            ydst = lambda c: y_s[0, :, :]
        xt_cur = front(sq, xsrc(0), T)
        for c in range(nch):
            hook = None
            if c + 1 < nch:
                hook = (lambda c=c, sq=sq, T=T: front(sq, xsrc(c + 1), T))
            xt_cur = chunk(sq, xt_cur, ydst(c), T, hook)
        hg_o = (hg_p if kind == "p" else hg_s)[bi]
        ssm_o = (ssm_p if kind == "p" else ssm_s)[bi]
        conv_o = (conv_p if kind == "p" else conv_s)[bi]
        P.dma(hg_o.re("h k v -> k h v"), S, "so_hg", eng="pool")
        for half in range(2):
            pb = pbank()
            for a in range(8):
                P.tr(pb[:, a * 128:(a + 1) * 128], H[:, half * 1024 + a * 128:half * 1024 + (a + 1) * 128], ident)
            tmp = scr[0]
            P.copy(tmp, pb)
            P.dma(ssm_o[half * 1024:(half + 1) * 1024, :].re("(a p) n -> p a n", p=128), tmp.re("p (a n) -> p a n", a=8), "so_ssm", eng="pool")
        c72 = ctmp[1]
        P.copy(c72[:, 0:72].re("p (j a) -> p a j", j=3), xb[:, :, 0:3])
        pb = pbank()
        P.tr(pb[:72, 0:128], c72[:, 0:72], ident)
        c72o = scr[1]
        P.copy(c72o[:72, 0:128], pb[:72, 0:128])
        P.dma(conv_o, c72o[:72, 0:128], "so_cv", eng="pool")

    P.finish(["yout", "so_hg", "so_ssm", "so_cv"])
    return nc, P


_CACHE = {}


def _layout_inputs(inp, core, NP, TP, TS, n_cores):
    f = lambda a: np.ascontiguousarray(a, dtype=np.float32)
    seq_ids = list(range(core * NP, (core + 1) * NP))
    cvec = np.concatenate([inp["c_prompt"][seq_ids], inp["c_sample"][core:core + 1]], axis=0)
    return cvec, seq_ids


def kernel(**inp):
    NP, TP, TS, NC = 2, 4096, 16, 8
    f = lambda a: np.ascontiguousarray(a, dtype=np.float32)
    if "nc" not in _CACHE:
        _CACHE["nc"] = build(NP, TP, TS)
    nc, P = _CACHE["nc"]
    shared = {
        "w_ada": f(inp["w_ada"][0]), "b_ada": f(inp["b_ada"][0]),
        "norm1_w": f(inp["norm1_w"][0].reshape(8, 128)),
        "w_in": f(inp["w_in"][0]),
        "lbs": f(inp["hgrn_lower_bounds"].reshape(16, 128)),
        "hgw": f(inp["hgrn_norm_w"][0].reshape(8, 128)),
        "conv_w": f(inp["conv_w"][0].reshape(96, 128)),
        "conv_b": f(inp["conv_b"][0].reshape(24, 128)),
        "dt_bias": f(inp["dt_bias"][0]), "a_log": f(inp["a_log"][0]), "ssm_d": f(inp["ssm_d"][0]),
        "snw": f(inp["ssm_norm_w"][0].reshape(16, 128)),
        "w_a": f(inp["w_branch_a"][0]), "w_b": f(inp["w_branch_b"][0]), "w_o": f(inp["w_out"][0]),
        "norm2_w": f(inp["norm2_w"][0].reshape(8, 128)),
        "wq": f(inp["peer_wq"][0]),
        "keysT": f(np.concatenate([inp["peer_keys1"][0], inp["peer_keys2"][0]], axis=0).transpose(0, 2, 1)),
        "uT": f(inp["peer_u"][0].T), "pv": f(inp["peer_v"][0]),
        "fnw": f(inp["final_norm_w"]),
    }
    in_maps = []
    for c in range(NC):
        m = dict(shared)
        ids = list(range(c * NP, (c + 1) * NP))
        m["xp"] = f(inp["x_prompt"][ids])
        m["xs"] = f(inp["x_sample"][c:c + 1])
        m["cv"] = f(np.concatenate([inp["c_prompt"][ids], inp["c_sample"][c:c + 1]], axis=0).reshape((NP + 1) * 8, 128))
        m["st_hg"] = f(inp["state_hgrn"][0, c])
        m["st_ssm"] = f(inp["state_ssm"][0, c].reshape(2048, 128))
        m["st_conv"] = f(inp["state_conv"][0, c].reshape(72, 128))
        in_maps.append(m)
    res = run_bass_kernel_spmd(nc, in_maps, core_ids=list(range(NC)))
    R = res.results
    cat = lambda k: np.concatenate([r[k] for r in R], axis=0)
    y_p = cat("y_p")
    y_s = cat("y_s")
    hg_p = cat("hg_p")[None]
    ssm_p = cat("ssm_p").reshape(NC * NP, 32, 64, 128)[None]
    conv_p = cat("conv_p").reshape(NC * NP, 3, 3072)[None]
    hg_s = cat("hg_s")[None]
    ssm_s = cat("ssm_s").reshape(NC, 32, 64, 128)[None]
    conv_s = cat("conv_s").reshape(NC, 3, 3072)[None]
    return (y_p, y_s, hg_p, ssm_p, conv_p, hg_s, ssm_s, conv_s)
```
